# Optimizing a Trainium2 kernel written in Bass

```python
import math
import jax, jax.numpy as jnp
from jax import lax
import numpy as np

D_MODEL = 1024
BATCH = 8
SEQ = 2048
DEPTH = 4
DEC_BATCH = 32
DEC_SEQ = 4
PAST_LEN = 8192
PAGE_SIZE = 128

N_MIXERS = 3
N_POOL_LAYERS = (DEPTH + N_MIXERS - 1) // N_MIXERS
N_ATTN_LAYERS = (DEPTH + N_MIXERS - 2) // N_MIXERS
N_CONV_LAYERS = (DEPTH + N_MIXERS - 3) // N_MIXERS

POOL_WINDOWS = (2, 4, 8, 16)
POOL_GROUPS = len(POOL_WINDOWS)
POOL_GW = D_MODEL // POOL_GROUPS
POOL_STATE = max(POOL_WINDOWS) - 1

ATTN_GROUPS = ((128, 1), (512, 4), (2048, 16))
N_ATTN_GROUPS = len(ATTN_GROUPS)
HEAD_DIM = 64
H_G = D_MODEL // HEAD_DIM
QB = 128
N_BUCKETS = 32
MAX_DISTANCE = 2048

CONV_WIDTH = 31
D_FF = 4 * D_MODEL
EPS = 1e-6
NEG_INF = -1e30

kernel_name = "hybrid_pool_dilattn_conformer_decode_step"


def rmsnorm(x, g):
    x32 = x.astype(jnp.float32)
    y = x32 * lax.rsqrt(jnp.mean(x32 * x32, axis=-1, keepdims=True) + EPS) * g.astype(jnp.float32)
    return y.astype(x.dtype)


def squared_relu_mlp(h, w_up, w_down):
    a = jax.nn.relu(jnp.einsum('btd,df->btf', h, w_up))
    return jnp.einsum('btf,fd->btd', a * a, w_down)


def pool_mixer(h_new, h_prev, pos0, w_grp, scale):
    B, T, _ = h_new.shape
    n_prev = h_prev.shape[1]
    h_ext = jnp.concatenate([h_prev, h_new], axis=1).astype(jnp.float32)
    h_ext = h_ext.reshape(B, n_prev + T, POOL_GROUPS, POOL_GW)
    csum = jnp.concatenate([jnp.zeros((B, 1, POOL_GROUPS, POOL_GW), jnp.float32),
                            jnp.cumsum(h_ext, axis=1)], axis=1)
    win = jnp.array(POOL_WINDOWS, jnp.int32)
    t = jnp.arange(T, dtype=jnp.int32)
    end = n_prev + t + 1
    start = jnp.maximum(end[:, None] - win[None, :], 0)
    grp = jnp.arange(POOL_GROUPS)[None, :]
    wsum = csum[:, end] - csum[:, start, grp]
    count = jnp.minimum(pos0 + t[:, None] + 1, win[None, :]).astype(jnp.float32)
    pooled = wsum / count[None, :, :, None] - h_ext[:, n_prev:]
    y = jnp.einsum('btgc,gcd->btgd', pooled, w_grp.astype(jnp.float32)).reshape(B, T, D_MODEL)
    return (y * scale.astype(jnp.float32)).astype(h_new.dtype)


def t5_bucket(dist):
    max_exact = N_BUCKETS // 2
    df = jnp.maximum(dist, 1).astype(jnp.float32)
    large = max_exact + (jnp.log(df / max_exact) / math.log(MAX_DISTANCE / max_exact)
                         * (N_BUCKETS - max_exact)).astype(jnp.int32)
    large = jnp.minimum(large, N_BUCKETS - 1)
    return jnp.where(dist < max_exact, dist, large)


def softmax_lse(s, mask):
    s = jnp.where(mask, s, NEG_INF)
    m = jnp.max(s, axis=-1, keepdims=True)
    e = jnp.exp(s - m)
    l = jnp.sum(e, axis=-1, keepdims=True)
    return e / l, (m + jnp.log(l))[..., 0]


def dilated_attn_prompt(q, k, v, bias_tab, dil, n_steps):
    B, S, H, Dh = q.shape
    L = S // dil
    NB = -(-L // QB)
    Lp = NB * QB

    def sub(a):
        return a.reshape(B, L, dil, H, Dh).transpose(0, 2, 1, 3, 4)

    qs = jnp.pad(sub(q), ((0, 0), (0, 0), (0, Lp - L), (0, 0), (0, 0))).reshape(B, dil, NB, QB, H, Dh)

    def kblocks(a):
        a = jnp.pad(sub(a), ((0, 0), (0, 0), (QB, Lp - L), (0, 0), (0, 0))).reshape(B, dil, NB + 1, QB, H, Dh)
        return jnp.concatenate([a[:, :, :-1], a[:, :, 1:]], axis=3)

    kb, vb = kblocks(k), kblocks(v)
    qi = jnp.arange(QB)[:, None]
    kj = jnp.arange(2 * QB)[None, :]
    steps = qi - kj + QB
    blk = jnp.arange(NB)[:, None, None]
    mask = (steps >= 0) & (steps <= n_steps) & (blk * QB + kj >= QB)
    bias = bias_tab[t5_bucket(dil * jnp.clip(steps, 0, n_steps))].transpose(2, 0, 1)
    s = jnp.einsum('brnqhc,brnkhc->brnhqk', qs, kb, preferred_element_type=jnp.float32) * (HEAD_DIM ** -0.5)
    s = s + bias.astype(jnp.float32)
    p, lse = softmax_lse(s, mask[None, None, :, None])
    o = jnp.einsum('brnhqk,brnkhc->brnqhc', p, vb.astype(jnp.float32))
    o = o.reshape(B, dil, Lp, H, Dh)[:, :, :L].transpose(0, 2, 1, 3, 4).reshape(B, S, H, Dh)
    lse = lse.transpose(0, 1, 2, 4, 3).reshape(B, dil, Lp, H)[:, :, :L].transpose(0, 2, 1, 3).reshape(B, S, H)
    return o, lse


def dilated_attn_sample(q, k_ext, v_ext, n_cache, bias_tab, dil, n_steps):
    T = q.shape[1]
    idx = n_cache + jnp.arange(T)[:, None] - dil * jnp.arange(n_steps + 1)[None, :]
    valid = idx >= 0
    idx = jnp.maximum(idx, 0)
    kg = k_ext[:, idx]
    vg = v_ext[:, idx]
    bias = bias_tab[t5_bucket(dil * jnp.arange(n_steps + 1))].T
    s = jnp.einsum('bthc,btkhc->bthk', q, kg, preferred_element_type=jnp.float32) * (HEAD_DIM ** -0.5)
    s = s + bias.astype(jnp.float32)[None, None]
    p, lse = softmax_lse(s, valid[None, :, None, :])
    o = jnp.einsum('bthk,btkhc->bthc', p, vg.astype(jnp.float32))
    return o, lse


def merge_groups(outs, lses, w_o, dtype):
    wts = jax.nn.softmax(jnp.stack(lses, axis=0), axis=0)
    o = jnp.sum(wts[..., None] * jnp.stack(outs, axis=0), axis=0)
    B, T = o.shape[:2]
    return jnp.einsum('bte,ed->btd', o.reshape(B, T, H_G * HEAD_DIM).astype(dtype), w_o)


def project_qkv(h, w_qkv):
    B, T, _ = h.shape
    return jnp.einsum('btd,de->bte', h, w_qkv).reshape(B, T, N_ATTN_GROUPS, 3, H_G, HEAD_DIM)


def attn_prompt(h, w_qkv, w_o, rel_bias):
    T = h.shape[1]
    qkv = project_qkv(h, w_qkv)
    outs, lses, kv_new = [], [], []
    for g, (win, dil) in enumerate(ATTN_GROUPS):
        o, l = dilated_attn_prompt(qkv[:, :, g, 0], qkv[:, :, g, 1], qkv[:, :, g, 2],
                                   rel_bias[:, g * H_G:(g + 1) * H_G], dil, win // dil)
        outs.append(o)
        lses.append(l)
        kv_new.append(qkv[:, T - min(win, T):, g, 1:3])
    return merge_groups(outs, lses, w_o, h.dtype), kv_new


def attn_sample(h, caches, w_qkv, w_o, rel_bias):
    T = h.shape[1]
    qkv = project_qkv(h, w_qkv)
    outs, lses, kv_new = [], [], []
    for g, (win, dil) in enumerate(ATTN_GROUPS):
        cache = caches[g]
        n_cache = cache.shape[1]
        kv_ext = jnp.concatenate([cache, qkv[:, :, g, 1:3].astype(cache.dtype)], axis=1)
        o, l = dilated_attn_sample(qkv[:, :, g, 0], kv_ext[:, :, 0], kv_ext[:, :, 1], n_cache,
                                   rel_bias[:, g * H_G:(g + 1) * H_G], dil, win // dil)
        outs.append(o)
        lses.append(l)
        keep = min(win, n_cache + T)
        kv_new.append(kv_ext[:, n_cache + T - keep:])
    return merge_groups(outs, lses, w_o, h.dtype), kv_new


def conv_mixer(h_new, u_prev, w1, b1, wdw, bdw, ln_g, ln_b, w2, b2):
    a = jnp.einsum('btd,de->bte', h_new, w1) + b1
    u = a[..., :D_MODEL] * jax.nn.sigmoid(a[..., D_MODEL:])
    u_ext = jnp.concatenate([u_prev.astype(u.dtype), u], axis=1)
    y = lax.conv_general_dilated(u_ext, wdw[:, None, :].astype(u.dtype), window_strides=(1,), padding='VALID',
                                 dimension_numbers=('NWC', 'WIO', 'NWC'), feature_group_count=D_MODEL) + bdw
    y32 = y.astype(jnp.float32)
    mu = jnp.mean(y32, axis=-1, keepdims=True)
    var = jnp.mean(jnp.square(y32 - mu), axis=-1, keepdims=True)
    yn = (y32 - mu) * lax.rsqrt(var + EPS) * ln_g.astype(jnp.float32) + ln_b.astype(jnp.float32)
    z = jnp.einsum('btd,de->bte', jax.nn.silu(yn).astype(h_new.dtype), w2) + b2
    return z, u_ext[:, -(CONV_WIDTH - 1):]


def setup_inputs(seed: int = 0) -> dict:
    key = jax.random.key(seed)
    ks = jax.random.split(key, 24)

    def nrm(k, shape, scale):
        return jax.random.normal(k, shape, jnp.float32) * scale

    return {
        'x_prompt': nrm(ks[0], (BATCH, SEQ, D_MODEL), 1.0),
        'x_sample': nrm(ks[1], (DEC_BATCH, DEC_SEQ, D_MODEL), 1.0),
        'state_pool': nrm(ks[2], (N_POOL_LAYERS, DEC_BATCH, POOL_STATE, D_MODEL), 1.0),
        'cache_kv_g0': nrm(ks[3], (N_ATTN_LAYERS, DEC_BATCH, min(ATTN_GROUPS[0][0], PAST_LEN), 2, H_G, HEAD_DIM), 1.0),
        'cache_kv_g1': nrm(ks[4], (N_ATTN_LAYERS, DEC_BATCH, min(ATTN_GROUPS[1][0], PAST_LEN), 2, H_G, HEAD_DIM), 1.0),
        'cache_kv_g2': nrm(ks[5], (N_ATTN_LAYERS, DEC_BATCH, min(ATTN_GROUPS[2][0], PAST_LEN), 2, H_G, HEAD_DIM), 1.0),
        'state_conv': nrm(ks[6], (N_CONV_LAYERS, DEC_BATCH, CONV_WIDTH - 1, D_MODEL), 0.5),
        'norm_gains': 1.0 + nrm(ks[7], (DEPTH, 4, D_MODEL), 0.02),
        'rel_bias': nrm(ks[8], (N_BUCKETS, N_ATTN_GROUPS * H_G), 0.1),
        'pool_w': nrm(ks[9], (N_POOL_LAYERS, POOL_GROUPS, POOL_GW, POOL_GW), POOL_GW ** -0.5),
        'pool_scale': 1.0 + nrm(ks[10], (N_POOL_LAYERS, D_MODEL), 0.1),
        'attn_w_qkv': nrm(ks[11], (N_ATTN_LAYERS, D_MODEL, N_ATTN_GROUPS * 3 * H_G * HEAD_DIM), D_MODEL ** -0.5),
        'attn_w_o': nrm(ks[12], (N_ATTN_LAYERS, H_G * HEAD_DIM, D_MODEL), (H_G * HEAD_DIM) ** -0.5),
        'conv_w_pw1': nrm(ks[13], (N_CONV_LAYERS, D_MODEL, 2 * D_MODEL), D_MODEL ** -0.5),
        'conv_b_pw1': nrm(ks[14], (N_CONV_LAYERS, 2 * D_MODEL), 0.02),
        'conv_w_dw': nrm(ks[15], (N_CONV_LAYERS, CONV_WIDTH, D_MODEL), CONV_WIDTH ** -0.5),
        'conv_b_dw': nrm(ks[16], (N_CONV_LAYERS, D_MODEL), 0.02),
        'conv_ln_g': 1.0 + nrm(ks[17], (N_CONV_LAYERS, D_MODEL), 0.02),
        'conv_ln_b': nrm(ks[18], (N_CONV_LAYERS, D_MODEL), 0.02),
        'conv_w_pw2': nrm(ks[19], (N_CONV_LAYERS, D_MODEL, D_MODEL), D_MODEL ** -0.5),
        'conv_b_pw2': nrm(ks[20], (N_CONV_LAYERS, D_MODEL), 0.02),
        'ffn_w_up': nrm(ks[21], (DEPTH, D_MODEL, D_FF), D_MODEL ** -0.5),
        'ffn_w_down': nrm(ks[22], (DEPTH, D_FF, D_MODEL), D_FF ** -0.5),
    }


def reference(x_prompt, x_sample, state_pool, cache_kv_g0, cache_kv_g1, cache_kv_g2, state_conv,
              norm_gains, rel_bias, pool_w, pool_scale, attn_w_qkv, attn_w_o,
              conv_w_pw1, conv_b_pw1, conv_w_dw, conv_b_dw, conv_ln_g, conv_ln_b, conv_w_pw2, conv_b_pw2,
              ffn_w_up, ffn_w_down):
    xp, xs = x_prompt, x_sample
    pool_p, pool_s, conv_p, conv_s = [], [], [], []
    kv_p = [[] for _ in ATTN_GROUPS]
    kv_s = [[] for _ in ATTN_GROUPS]
    for i in range(DEPTH):
        kind = i % N_MIXERS
        j = i // N_MIXERS
        hp = rmsnorm(xp, norm_gains[i, 0])
        hs = rmsnorm(xs, norm_gains[i, 0])
        if kind == 0:
            mp = pool_mixer(hp, hp[:, :0], 0, pool_w[j], pool_scale[j])
            ms = pool_mixer(hs, state_pool[j].astype(hs.dtype), PAST_LEN, pool_w[j], pool_scale[j])
            pool_p.append(hp[:, -POOL_STATE:])
            pool_s.append(jnp.concatenate([state_pool[j].astype(hs.dtype), hs], axis=1)[:, -POOL_STATE:])
        elif kind == 1:
            mp, new_p = attn_prompt(hp, attn_w_qkv[j], attn_w_o[j], rel_bias)
            ms, new_s = attn_sample(hs, (cache_kv_g0[j], cache_kv_g1[j], cache_kv_g2[j]),
                                    attn_w_qkv[j], attn_w_o[j], rel_bias)
            for g in range(N_ATTN_GROUPS):
                kv_p[g].append(new_p[g])
                kv_s[g].append(new_s[g])
        else:
            cw = (conv_w_pw1[j], conv_b_pw1[j], conv_w_dw[j], conv_b_dw[j],
                  conv_ln_g[j], conv_ln_b[j], conv_w_pw2[j], conv_b_pw2[j])
            mp, up = conv_mixer(hp, jnp.zeros((hp.shape[0], CONV_WIDTH - 1, D_MODEL), hp.dtype), *cw)
            ms, us = conv_mixer(hs, state_conv[j], *cw)
            conv_p.append(up)
            conv_s.append(us)
        xp = xp + rmsnorm(mp, norm_gains[i, 1])
        xs = xs + rmsnorm(ms, norm_gains[i, 1])
        xp = xp + rmsnorm(squared_relu_mlp(rmsnorm(xp, norm_gains[i, 2]), ffn_w_up[i], ffn_w_down[i]), norm_gains[i, 3])
        xs = xs + rmsnorm(squared_relu_mlp(rmsnorm(xs, norm_gains[i, 2]), ffn_w_up[i], ffn_w_down[i]), norm_gains[i, 3])
    return (xp, xs,
            jnp.stack(pool_p), jnp.stack(pool_s),
            jnp.stack(kv_p[0]), jnp.stack(kv_s[0]),
            jnp.stack(kv_p[1]), jnp.stack(kv_s[1]),
            jnp.stack(kv_p[2]), jnp.stack(kv_s[2]),
            jnp.stack(conv_p), jnp.stack(conv_s))
```

```python
import numpy as np
import concourse.bass as bass
import concourse.mybir as mybir
from concourse.bass_utils import run_bass_kernel_spmd

F32 = mybir.dt.float32
BF16 = mybir.dt.bfloat16
ALU = mybir.AluOpType
AF = mybir.ActivationFunctionType

D = 1024
NCH = 8
S = 2048
NS = 16
T = S + NS
DFF = 4096
EPS = 1e-6
PADL = 32
NCORES = 8
ENGS = ("pe", "act", "dve", "pool", "sp")
NDMASEM = 64
DEBUG = {}
SB_BASE = 16640
SB_SIZE = 229376 - SB_BASE


def call(name, *a, **k):
    return (name, a, k)


class Sched:
    EPOCH = 12000

    def __init__(self, nc):
        self.nc = nc
        self.prog = {e: [] for e in ENGS}
        self.cnt = {e: 0 for e in ENGS}
        self.esems = {e: [] for e in ENGS}
        self.seen = {e: {} for e in ENGS}
        self.lastw = {}
        self.readers = {}
        self.dsems = [nc.alloc_semaphore(f"dq{i}") for i in range(NDMASEM)]
        self.dcnt = [0] * NDMASEM
        self.dpool = {"sp": list(range(0, 24)), "pool": list(range(24, 40)), "act": list(range(40, NDMASEM))}
        self.drr = {"sp": 0, "pool": 0, "act": 0}
        self.pe_pending = []
        self.nops = 0

    def _esem(self, e, ep):
        while len(self.esems[e]) <= ep:
            self.esems[e].append(self.nc.alloc_semaphore(f"e_{e}_{len(self.esems[e])}"))
        return self.esems[e][ep]

    def _handle(self, key):
        if key[0] == "c":
            return self._esem(key[1], key[2])
        return self.dsems[key[1]]

    def _need_wait(self, eng, tok):
        key, val = tok
        if key[0] == "c":
            k2 = ("c", key[1])
            cur = self.seen[eng].get(k2, (-1, 0))
            return (key[2], val) > cur
        cur = self.seen[eng].get(key, 0)
        return val > cur

    def _mark(self, eng, tok):
        key, val = tok
        if key[0] == "c":
            self.seen[eng][("c", key[1])] = (key[2], val)
        else:
            self.seen[eng][key] = val

    def _wait(self, eng, tok):
        if tok is None or not self._need_wait(eng, tok):
            return
        self.prog[eng].append(("w", self._handle(tok[0]), tok[1]))
        self._mark(eng, tok)

    def _deps(self, eng, reads, writes, is_dma):
        deps = []
        for r in reads:
            t = self.lastw.get(r)
            if t is not None:
                deps.append((t, "raw"))
        for w in writes:
            t = self.lastw.get(w)
            if t is not None:
                deps.append((t, "waw"))
            for t in self.readers.get(w, ()):
                deps.append((t, "war"))
        out = []
        for t, kind in deps:
            key = t[0]
            if (not is_dma) and key[0] == "c" and key[1] == eng:
                if eng == "pe" or kind != "raw":
                    continue
            out.append(t)
        return out

    def _record(self, tok, reads, writes):
        for r in reads:
            self.readers.setdefault(r, []).append(tok)
        for w in writes:
            self.lastw[w] = tok
            self.readers[w] = []

    def op(self, eng, fn, reads=(), writes=(), signal=True):
        self.nops += 1
        reads = tuple(reads)
        writes = tuple(writes)
        for t in self._deps(eng, reads, writes, False):
            self._wait(eng, t)
        if not signal:
            assert eng == "pe"
            self.prog[eng].append(("o", fn, None, 0))
            self.pe_pending.append((reads, writes))
            return None
        self.cnt[eng] += 1
        ep, val = divmod(self.cnt[eng] - 1, self.EPOCH)
        val += 1
        tok = (("c", eng, ep), val)
        self.prog[eng].append(("o", fn, self._esem(eng, ep), 1))
        if eng == "pe":
            for (r, w) in self.pe_pending:
                self._record(tok, r, w)
            self.pe_pending = []
        self._record(tok, reads, writes)
        return tok

    def dma(self, q, fn, reads=(), writes=()):
        self.nops += 1
        reads = tuple(reads)
        writes = tuple(writes)
        pl = self.dpool[q]
        i = pl[self.drr[q] % len(pl)]
        self.drr[q] += 1
        if self.dcnt[i] > 0:
            self._wait(q, (("d", i), 16 * self.dcnt[i]))
        for t in self._deps(q, reads, writes, True):
            self._wait(q, t)
        self.dcnt[i] += 1
        tok = (("d", i), 16 * self.dcnt[i])
        self.prog[q].append(("o", fn, self.dsems[i], 16))
        self._record(tok, reads, writes)
        return tok

    def barrier(self):
        assert not self.pe_pending
        toks = []
        for e in ENGS:
            if e != "sp" and self.cnt[e] > 0:
                ep, val = divmod(self.cnt[e] - 1, self.EPOCH)
                toks.append((("c", e, ep), val + 1))
        for i in range(NDMASEM):
            if self.dcnt[i] > 0:
                toks.append((("d", i), 16 * self.dcnt[i]))
        for e in ENGS:
            for t in toks:
                self._wait(e, t)

    def finish(self):
        for i in range(NDMASEM):
            if self.dcnt[i] > 0:
                self._wait("sp", (("d", i), 16 * self.dcnt[i]))
        for e in ENGS:
            if e != "sp" and self.cnt[e] > 0:
                ep, val = divmod(self.cnt[e] - 1, self.EPOCH)
                self._wait("sp", (("c", e, ep), val + 1))

    def emit(self):
        nc = self.nc
        with nc.Block() as block:
            for e, deco in (("pe", block.tensor), ("act", block.scalar), ("dve", block.vector),
                            ("pool", block.gpsimd), ("sp", block.sync)):
                items = self.prog[e]

                def body(eng, items=items):
                    for it in items:
                        if it[0] == "w":
                            eng.wait_ge(it[1], it[2])
                        else:
                            nm, a, k = it[1]
                            ins = getattr(eng, nm)(*a, **k)
                            if it[2] is not None:
                                ins.then_inc(it[2], it[3])
                deco(body)


class SB:
    def __init__(self, nc):
        self.nc = nc
        self.n = 0

    def at(self, off, shape, dtype, name=None):
        self.n += 1
        return self.nc.alloc_sbuf_tensor_at(name or f"sb{self.n}", list(shape), dtype, offset=off + SB_BASE)


def nbytes(shape, dtype):
    n = 1
    for s in shape[1:]:
        n *= s
    return n * (4 if dtype == F32 else 2)


class Region:
    def __init__(self, sb, start, end, sch=None):
        self.sb, self.start, self.end, self.cur = sb, start, end, start
        self.sch = sch

    def alloc(self, shape, dtype, name=None):
        off = (self.cur + 63) // 64 * 64
        nb = nbytes(shape, dtype)
        assert off + nb <= self.end, (name, off, nb, self.end)
        self.cur = off + nb
        return self.sb.at(off, shape, dtype, name)

    def reset(self):
        self.cur = self.start
        if self.sch is not None:
            self.sch.barrier()

    def mark(self):
        return self.cur

    def release(self, mark):
        self.cur = mark
        if self.sch is not None:
            self.sch.barrier()


V_GAIN = 0
V_PSCALE = 16
V_B1 = 18
V_BDW = 20
V_LNG = 21
V_LNB = 22
V_B2 = 23
V_WDW = 24
NVEC = 64


def build_program(layers=(0, 1, 2, 3)):
    nc = bass.Bass("TRN2", target_bir_lowering=False)
    sch = Sched(nc)

    def din(name, shape, dt=F32):
        return nc.dram_tensor(name, list(shape), dt, kind="ExternalInput")

    def dout(name, shape, dt=F32):
        return nc.dram_tensor(name, list(shape), dt, kind="ExternalOutput")

    x_p = din("x_p", [S, D])
    x_s = din("x_s", [NS, D])
    st_pool = din("st_pool", [2, 4, 15, D])
    vec_rows = din("vec_rows", [NVEC, D])
    ident_f_d = din("ident_f", [128, 128])
    invcnt_d = din("invcnt", [128, 4, 16])
    pool_w = din("pool_w", [2, 4, 256, 256])
    w_up = din("w_up", [4, D, DFF])
    w_down = din("w_down", [4, DFF, D])

    y_p = dout("y_p", [S, D])
    y_s = dout("y_s", [NS, D])
    o_pool_p = dout("o_pool_p", [2, 15, D])
    o_pool_s = dout("o_pool_s", [2, 4, 15, D])
    st_conv = din("st_conv", [4, 30, D])
    w_pw1 = din("w_pw1", [D, 2 * D])
    w_pw2 = din("w_pw2", [D, D])
    o_conv_p = dout("o_conv_p", [30, D])
    o_conv_s = dout("o_conv_s", [4, 30, D])
    WINS = (128, 512, 2048)
    DILS = (1, 4, 16)
    RS = 2 * D
    w_qkv = din("w_qkv", [D, 9 * D])
    w_o = din("w_o", [D, D])
    rel_bias_d = din("rel_bias", [32, 48])
    ohf_d = din("ohf", [32, 3, 384])
    ohm_d = din("ohm", [32, 3, 4, 128])
    ohx_d = din("ohx", [32, 3, 4, NS])
    selc_d = din("selc", [128, NS, NS])
    selr_d = din("selr", [NS, NS, 128])
    cache_kv = [din(f"cache_kv{g}", [4, WINS[g], RS]) for g in range(3)]
    o_kv_p = [dout(f"o_kv_p{g}", [WINS[g], RS]) for g in range(3)]
    o_kv_s = [dout(f"o_kv_s{g}", [4, WINS[g], RS]) for g in range(3)]
    xt_scr = nc.dram_tensor("xt_scr", [128, NCH * T], F32)
    escr = nc.dram_tensor("escr", [3, 16, 128, 384], BF16)
    qkvs_scr = nc.dram_tensor("qkvs_scr", [NS, 9 * D], F32)

    sb = SB(nc)
    off = 0

    def persist(shape, dtype, name):
        nonlocal off
        off = (off + 63) // 64 * 64
        t = sb.at(off, shape, dtype, name)
        off += nbytes(shape, dtype)
        return t

    XT = persist([128, NCH, T], F32, "XT")
    ident_f = persist([128, 128], F32, "ident_f")
    ident_b = persist([128, 128], BF16, "ident_b")
    ones_m = persist([128, 128], BF16, "ones_m")
    vecs = persist([128, NCH, NVEC], F32, "vecs")
    invcnt = persist([128, 4, 16], F32, "invcnt")
    epsc = persist([128, 16], F32, "epsc")
    rstd = [persist([128, 528], F32, f"rstd{i}") for i in range(2)]
    WR_SLOTS = 4
    WR_BYTES = 8192
    wring_off = (off + 63) // 64 * 64
    off = wring_off + WR_SLOTS * WR_BYTES
    PH0 = (off + 63) // 64 * 64
    PH_END = SB_SIZE
    reg = Region(sb, PH0, PH_END, sch)

    PS = [nc.alloc_psum_tensor(f"ps{i}", [128, 512], F32) for i in range(8)]
    ps_rr = [0]
    ps_reserved = set()

    def next_bank():
        while True:
            b = ps_rr[0]
            ps_rr[0] = (b + 1) % 8
            if b not in ps_reserved:
                return b

    wr_rr = [0]

    def wslot(shape, dtype=BF16):
        i = wr_rr[0]
        wr_rr[0] = (i + 1) % WR_SLOTS
        assert nbytes(shape, dtype) <= WR_BYTES
        return i, sb.at(wring_off + i * WR_BYTES, shape, dtype)

    def gvec(row, c):
        return vecs[:, c, row:row + 1]

    dumps = {}

    def dump(name, t, shape, res):
        if not DEBUG.get("dump"):
            return
        d = nc.dram_tensor("dbg_" + name, list(shape), F32, kind="ExternalOutput")
        dumps[name] = d
        nd = len(shape)
        sl = tuple(slice(None) for _ in range(nd))
        sch.dma("sp", call("dma_start", out=d[sl], in_=t[sl]), reads=[res], writes=["dbg_" + name])

    sch.dma("sp", call("dma_start", out=ident_f[:, :], in_=ident_f_d[:, :]), writes=["ident_f"])
    sch.dma("sp", call("dma_start", out=invcnt[:, :, :], in_=invcnt_d[:, :, :]), writes=["invcnt"])
    sch.op("dve", call("tensor_copy", out=ident_b[:, :], in_=ident_f[:, :]), reads=["ident_f"], writes=["ident_b"])
    sch.op("pool", call("memset", ones_m[:, :], 1.0 / D), writes=["ones_m"])
    sch.op("pool", call("memset", epsc[:, :], EPS), writes=["epsc"])

    reg.reset()
    vstage = reg.alloc([NVEC, D], F32, "vstage")
    sch.dma("sp", call("dma_start", out=vstage[:, :], in_=vec_rows[:, :]), writes=["vstage"])
    for c in range(NCH):
        b = next_bank()
        sch.op("pe", call("transpose", out=PS[b][:, 0:NVEC], in_=vstage[:, c * 128:(c + 1) * 128],
                                                     identity=ident_f[0:NVEC, 0:NVEC]),
               reads=["vstage", "ident_f"], writes=[f"ps{b}"])
        sch.op("dve", call("tensor_copy", out=vecs[:, c, :], in_=PS[b][:, 0:NVEC]),
               reads=[f"ps{b}"], writes=["vecs"])

    xstage = [reg.alloc([128, D], F32, f"xstage{i}") for i in range(2)]

    def load_block(src_ap, nrows, col0, i):
        st = xstage[i % 2]
        sch.dma("sp", call("dma_start", out=st[0:nrows, :], in_=src_ap), writes=[f"xstage{i % 2}"])
        for c0 in range(0, NCH, 4):
            b = next_bank()
            for c in range(c0, c0 + 4):
                sch.op("pe", call("transpose",
                    out=PS[b][:, (c - c0) * 128:(c - c0) * 128 + nrows], in_=st[0:nrows, c * 128:(c + 1) * 128],
                    identity=ident_f[0:nrows, 0:nrows]),
                    reads=[f"xstage{i % 2}", "ident_f"], writes=[f"ps{b}"], signal=(c == c0 + 3))
            eng = "dve" if (c0 == 0) else "act"
            if eng == "dve":
                sch.op("dve", call("tensor_copy",
                    out=XT[:, c0:c0 + 4, col0:col0 + nrows],
                    in_=PS[b][:, :].rearrange("p (c t) -> p c t", c=4)[:, :, 0:nrows]),
                    reads=[f"ps{b}"], writes=["XT"])
            else:
                sch.op("act", call("copy",
                    out=XT[:, c0:c0 + 4, col0:col0 + nrows],
                    in_=PS[b][:, :].rearrange("p (c t) -> p c t", c=4)[:, :, 0:nrows]),
                    reads=[f"ps{b}"], writes=["XT"])

    for i in range(S // 128):
        load_block(x_p[i * 128:(i + 1) * 128, :], 128, i * 128, i)
    load_block(x_s[:, :], NS, S, 16)

    TILES = [[(0, 512)], [(512, 512)], [(1024, 512)], [(1536, 512), (2048, NS)]]
    if DEBUG.get("no_segB"):
        TILES[3] = [(1536, 512)]

    def sumsq_rstd(src_fn, segs, sq, rs, res_reads, eps=EPS):
        o = 0
        for si, (c0, n) in enumerate(segs):
            for c in range(NCH):
                sch.op("act", call("activation", out=sq[:, c, o:o + n], in_=src_fn(c, si), func=AF.Square),
                       reads=(res_reads(c) if callable(res_reads) else res_reads), writes=[sq.name + str(c)])
            b = next_bank()
            for c in range(NCH):
                sch.op("pe", call("matmul", PS[b][:, 0:n], lhsT=ones_m[:, :], rhs=sq[:, c, o:o + n],
                                                                  start=(c == 0), stop=(c == NCH - 1)),
                       reads=[sq.name + str(c), "ones_m"], writes=[f"ps{b}"], signal=(c == NCH - 1))
            if DEBUG.get("arsqrt", False):
                sch.op("act", call("activation", out=rs[:, o:o + n], in_=PS[b][:, 0:n], func=AF.Abs_reciprocal_sqrt,
                                   bias=epsc[:, 0:1], scale=1.0), reads=[f"ps{b}", "epsc"], writes=[rs.name])
            else:
                sch.op("act", call("activation", out=rs[:, o:o + n], in_=PS[b][:, 0:n], func=AF.Sqrt,
                                   bias=epsc[:, 0:1], scale=1.0), reads=[f"ps{b}", "epsc"], writes=[rs.name])
                sch.op("dve", call("reciprocal", out=rs[:, o:o + n], in_=rs[:, o:o + n]),
                       reads=[rs.name], writes=[rs.name])
            o += n

    def prenorm(gain_row, segs, sq, rs, out_fn, out_res):
        sumsq_rstd(lambda c, si: XT[:, c, segs[si][0]:segs[si][0] + segs[si][1]], segs, sq, rs, ["XT"])
        o = 0
        for si, (c0, n) in enumerate(segs):
            for c in range(NCH):
                sch.op("dve", call("scalar_tensor_tensor",
                    out=out_fn(c, si), in0=XT[:, c, c0:c0 + n], scalar=gvec(V_GAIN + gain_row, c),
                    in1=rs[:, o:o + n], op0=ALU.mult, op1=ALU.mult),
                    reads=["XT", rs.name, "vecs"], writes=[out_res(c) if callable(out_res) else out_res])
            o += n

    def postnorm_residual(gain_row, segs, mst, sq, rs, mres=None):
        if mres is None:
            mres = lambda c: "mst"
        offs = []
        o = 0
        for (c0, n) in segs:
            offs.append(o)
            o += n
        sumsq_rstd(lambda c, si: mst[:, c, offs[si]:offs[si] + segs[si][1]], segs, sq, rs, lambda c: [mres(c)])
        for si, (c0, n) in enumerate(segs):
            o = offs[si]
            for c in range(NCH):
                eng = "dve"
                sch.op(eng, call("tensor_tensor", out=mst[:, c, o:o + n], in0=mst[:, c, o:o + n],
                                                                    in1=rs[:, o:o + n], op=ALU.mult),
                       reads=[mres(c), rs.name], writes=[mres(c)])
                sch.op("dve", call("scalar_tensor_tensor",
                    out=XT[:, c, c0:c0 + n], in0=mst[:, c, o:o + n], scalar=gvec(V_GAIN + gain_row, c),
                    in1=XT[:, c, c0:c0 + n], op0=ALU.mult, op1=ALU.add),
                    reads=[mres(c), "vecs", "XT"], writes=["XT"])

    def load_w(dram_ap, shape):
        i, t = wslot(shape)
        sch.dma("pool", call("dma_start", out=t[:, :, :], in_=dram_ap), writes=[f"wr{i}"])
        return i, t

    class WStream:
        def __init__(self, pieces):
            self.pieces = pieces
            self.issued = 0
            self.slots = {}

        def get(self, i, pf=WR_SLOTS - 1):
            upto = min(i + 1 + pf, len(self.pieces))
            while self.issued < upto:
                ap, shape = self.pieces[self.issued]
                self.slots[self.issued] = load_w(ap, shape)
                self.issued += 1
            return self.slots.pop(i)

    def ffn(layer):
        reg.reset()
        aT = reg.alloc([128, 32, 528], BF16, "aT")
        h2 = [reg.alloc([128, NCH, 528], BF16, f"h2T{i}") for i in range(2)]
        mst = reg.alloc([128, NCH, 528], F32, "mst")
        sqA = reg.alloc([128, NCH, 528], BF16, "sqA")
        sqB = reg.alloc([128, NCH, 528], BF16, "sqB")
        rl = [reg.alloc([128, 528], BF16, f"rl{i}") for i in range(3)]
        wu = w_up[layer].rearrange("(k p) f -> p k f", p=128)
        wd = w_down[layer].rearrange("(f p) d -> p f d", p=128)
        tiles = [ti for ti in range(4) if DEBUG.get("ffn_tiles") is None or ti in DEBUG["ffn_tiles"]]
        pieces = []
        for ti in tiles:
            for fp in range(8):
                pieces.append((wu[:, :, fp * 512:(fp + 1) * 512], [128, NCH, 512]))
            for dh in range(2):
                for fp in range(4):
                    pieces.append((wd[:, fp * 8:(fp + 1) * 8, dh * 512:(dh + 1) * 512], [128, 8, 512]))
        ws = WStream(pieces)
        pc = [0]
        offs = [0, 512]

        def do_prenorm(ti):
            segs = TILES[ti]
            h2T = h2[ti % 2]
            views = [[h2T[:, c, offs[si]:offs[si] + n] for si, (c0, n) in enumerate(segs)] for c in range(NCH)]
            prenorm(layer * 4 + 2, segs, sqA, rstd[ti % 2], lambda c, si, v=views: v[c][si], h2T.name)

        do_prenorm(tiles[0])
        rln = [0]
        for idx, ti in enumerate(tiles):
            segs = TILES[ti]
            h2T = h2[ti % 2]
            hasB = len(segs) > 1
            bankB = next_bank() if hasB else None
            if hasB:
                ps_reserved.add(bankB)
            for fp in range(8):
                wi, wt = ws.get(pc[0])
                pc[0] += 1
                for fl in range(4):
                    fc = fp * 4 + fl
                    b = next_bank()
                    for k in range(NCH):
                        sch.op("pe", call("matmul", PS[b][:, 0:512], lhsT=wt[:, k, fl * 128:(fl + 1) * 128], rhs=h2T[:, k, 0:512],
                                          start=(k == 0), stop=(k == NCH - 1)),
                               reads=[f"wr{wi}", h2T.name], writes=[f"ps{b}"], signal=(k == NCH - 1))
                    if hasB:
                        for k in range(NCH):
                            sch.op("pe", call("matmul", PS[bankB][:, fc * NS:(fc + 1) * NS], lhsT=wt[:, k, fl * 128:(fl + 1) * 128],
                                              rhs=h2T[:, k, 512:512 + NS], start=(k == 0), stop=(k == NCH - 1)),
                                   reads=[f"wr{wi}", h2T.name], writes=[f"ps{bankB}"], signal=(k == NCH - 1))
                    r = rl[rln[0] % 3]
                    rln[0] += 1
                    sch.op("act", call("activation", out=r[:, 0:512], in_=PS[b][:, 0:512], func=AF.Relu),
                           reads=[f"ps{b}"], writes=[r.name])
                    sch.op("dve", call("tensor_tensor", out=aT[:, fc, 0:512], in0=r[:, 0:512], in1=r[:, 0:512], op=ALU.mult),
                           reads=[r.name], writes=["aT"])
            if hasB:
                r = rl[rln[0] % 3]
                rln[0] += 1
                sch.op("act", call("activation", out=r[:, 0:512], in_=PS[bankB][:, 0:512], func=AF.Relu),
                       reads=[f"ps{bankB}"], writes=[r.name])
                sch.op("dve", call("tensor_tensor", out=aT[:, :, 512:512 + NS],
                                   in0=r[:, 0:512].rearrange("p (f t) -> p f t", t=NS),
                                   in1=r[:, 0:512].rearrange("p (f t) -> p f t", t=NS), op=ALU.mult),
                       reads=[r.name], writes=["aT"])
                ps_reserved.discard(bankB)
            if idx + 1 < len(tiles):
                do_prenorm(tiles[idx + 1])
            for dh in range(2):
                banks = [next_bank() for _ in range(4)]
                bBs = [next_bank() for _ in range(4)] if hasB else None
                for fp in range(4):
                    wi, wt = ws.get(pc[0])
                    pc[0] += 1
                    for dl in range(4):
                        b = banks[dl]
                        for fl in range(8):
                            f = fp * 8 + fl
                            sch.op("pe", call("matmul", PS[b][:, 0:512], lhsT=wt[:, fl, dl * 128:(dl + 1) * 128], rhs=aT[:, f, 0:512],
                                              start=(f == 0), stop=(f == 31)),
                                   reads=[f"wr{wi}", "aT"], writes=[f"ps{b}"], signal=(fl == 7))
                        if hasB:
                            for fl in range(8):
                                f = fp * 8 + fl
                                sch.op("pe", call("matmul", PS[bBs[dl]][:, 0:NS], lhsT=wt[:, fl, dl * 128:(dl + 1) * 128],
                                                  rhs=aT[:, f, 512:512 + NS], start=(f == 0), stop=(f == 31)),
                                       reads=[f"wr{wi}", "aT"], writes=[f"ps{bBs[dl]}"], signal=(fl == 7))
                for dl in range(4):
                    b = banks[dl]
                    c = dh * 4 + dl
                    if dl % 2 == 0:
                        sch.op("act", call("copy", out=mst[:, c, 0:512], in_=PS[b][:, 0:512]), reads=[f"ps{b}"], writes=[f"mst{c}"])
                    else:
                        sch.op("dve", call("tensor_copy", out=mst[:, c, 0:512], in_=PS[b][:, 0:512]), reads=[f"ps{b}"], writes=[f"mst{c}"])
                if hasB:
                    for dl in range(4):
                        sch.op("dve", call("tensor_copy", out=mst[:, dh * 4 + dl, 512:512 + NS], in_=PS[bBs[dl]][:, 0:NS]),
                               reads=[f"ps{bBs[dl]}"], writes=[f"mst{dh * 4 + dl}"])
            postnorm_residual(layer * 4 + 3, segs, mst, sqB, rstd[ti % 2], mres=lambda c: f"mst{c}")

    def pool_layer(layer, j):
        reg.reset()
        hT = reg.alloc([128, NCH, PADL + T], BF16, "hT")
        pooled = hT[:, :, PADL:PADL + T]
        tA = reg.alloc([128, 16 + S], F32, "tA")
        tB = reg.alloc([128, 16 + S], F32, "tB")
        tC = reg.alloc([128, 16 + S], F32, "tC")
        tD = reg.alloc([128, 16 + S], F32, "tD")
        corr2 = reg.alloc([128, 16], F32, "corr2")
        hse = reg.alloc([128, NCH, 4, 19], F32, "hse")
        corr = reg.alloc([128, 16], F32, "corr")
        sq = reg.alloc([128, NCH, 528], BF16, "sq")
        mst = None
        sch.op("pool", call("memset", hT[:, :, 0:PADL], 0.0), writes=[f"hTc{c}" for c in range(NCH)])
        for ti, segs in enumerate(TILES):
            rs = rstd[ti % 2]
            prenorm(layer * 4 + 0, segs, sq, rs,
                    lambda c, si: hT[:, c, PADL + segs[si][0]:PADL + segs[si][0] + segs[si][1]], lambda c: f"hTc{c}")
            if ti == 3:
                for c in range(NCH):
                    sch.op("dve", call("scalar_tensor_tensor",
                        out=hse[:, c, :, 15:19], in0=XT[:, c, S:S + NS].rearrange("p (b t) -> p b t", t=4),
                        scalar=gvec(V_GAIN + layer * 4, c), in1=rs[:, 512:512 + NS].rearrange("p (b t) -> p b t", t=4),
                        op0=ALU.mult, op1=ALU.mult), reads=["XT", rs.name, "vecs"], writes=["hse"])
                hp15 = reg.alloc([128, NCH, 16], F32, "hp15")
                for c in range(NCH):
                    sch.op("dve", call("scalar_tensor_tensor",
                        out=hp15[:, c, :], in0=XT[:, c, S - 16:S], scalar=gvec(V_GAIN + layer * 4, c),
                        in1=rs[:, 512 - 16:512], op0=ALU.mult, op1=ALU.mult),
                        reads=["XT", rs.name, "vecs"], writes=["hp15"])
        dump(f"rs{layer}", rstd[1], [128, 528], "rstd1")
        dump(f"hp15_{layer}", hp15, [128, NCH, 16], "hp15")
        dump(f"vecs{layer}", vecs, [128, NCH, NVEC], "vecs")
        sst = reg.alloc([128, D], F32, "sst")
        sch.dma("sp", call("dma_start", out=sst[0:60, :], in_=st_pool[j].rearrange("b r d -> (b r) d")),
                writes=["sst"])
        for c in range(NCH):
            b = next_bank()
            sch.op("pe", call("transpose", out=PS[b][:, 0:60], in_=sst[0:60, c * 128:(c + 1) * 128],
                                                         identity=ident_f[0:60, 0:60]),
                   reads=["sst", "ident_f"], writes=[f"ps{b}"])
            sch.op("act", call("copy", out=hse[:, c, :, 0:15],
                                                    in_=PS[b][:, 0:60].rearrange("p (b r) -> p b r", r=15)),
                   reads=[f"ps{b}"], writes=["hse"])
        ost = sst
        b = next_bank()
        b2 = next_bank()
        for c in range(NCH):
            bb = b if c < 4 else b2
            sch.op("pe", call("transpose", out=PS[bb][0:15, (c % 4) * 128:(c % 4 + 1) * 128],
                                                           in_=hp15[:, c, 1:16], identity=ident_f[:, :]),
                   reads=["hp15", "ident_f"], writes=[f"ps{bb}"], signal=(c % 4 == 3))
        sch.op("dve", call("tensor_copy", out=ost[0:15, 0:512], in_=PS[b][0:15, :]), reads=[f"ps{b}"], writes=["sst"])
        sch.op("dve", call("tensor_copy", out=ost[0:15, 512:1024], in_=PS[b2][0:15, :]), reads=[f"ps{b2}"], writes=["sst"])
        sch.dma("sp", call("dma_start", out=o_pool_p[j, :, :], in_=ost[0:15, :]), reads=["sst"], writes=["o_pool_p"])
        ost2 = sst
        hsc = reg.alloc([128, NCH, 60], F32, "hsc")
        for c in range(NCH):
            sch.op("dve", call("tensor_copy", out=hsc[:, c, :].rearrange("p (b r) -> p b r", r=15),
                                                       in_=hse[:, c, :, 4:19]), reads=["hse"], writes=["hsc"])
        b = next_bank()
        b2 = next_bank()
        for c in range(NCH):
            bb = b if c < 4 else b2
            sch.op("pe", call("transpose", out=PS[bb][0:60, (c % 4) * 128:(c % 4 + 1) * 128],
                                                           in_=hsc[:, c, :], identity=ident_f[:, :]),
                   reads=["hsc", "ident_f"], writes=[f"ps{bb}"], signal=(c % 4 == 3))
        sch.op("dve", call("tensor_copy", out=ost2[0:60, 0:512], in_=PS[b][0:60, :]), reads=[f"ps{b}"], writes=["sst"])
        sch.op("dve", call("tensor_copy", out=ost2[0:60, 512:1024], in_=PS[b2][0:60, :]), reads=[f"ps{b2}"], writes=["sst"])
        sch.dma("sp", call("dma_start", out=o_pool_s[j].rearrange("b r d -> (b r) d"), in_=ost2[0:60, :]),
                reads=["sst"], writes=["o_pool_s"])
        dump(f"hse{layer}", hse, [128, NCH, 4, 19], "hse")
        dump(f"hsc{layer}", hsc, [128, NCH, 60], "hsc")
        dump(f"ost2_{layer}", ost2, [64, D], "sst")
        dump(f"sst{layer}", sst, [64, D], "sst")
        for c in range(NCH):
            g = c // 2
            w = 2 << g
            eng = "pool" if c < 3 else "dve"
            hp = hT[:, c, PADL - 16:PADL + S]
            cur = None
            bufs = [tA, tB] if eng == "dve" else [tC, tD]
            res = ["tA", "tB"] if eng == "dve" else ["tC", "tD"]
            corr_ = corr if eng == "dve" else corr2
            corr_n = "corr" if eng == "dve" else "corr2"
            sh = 1
            step = 0
            while sh < w:
                dst = bufs[step % 2]
                if cur is None:
                    sch.op(eng, call("tensor_tensor",
                        out=dst[:, 16:16 + S], in0=hp[:, 16:16 + S], in1=hp[:, 16 - sh:16 - sh + S], op=ALU.add),
                        reads=[f"hTc{c}"], writes=[res[step % 2]])
                    sch.op(eng, call("memset", dst[:, 0:16], 0.0), writes=[res[step % 2]])
                else:
                    src = cur
                    sch.op(eng, call("tensor_tensor",
                        out=dst[:, 16:16 + S], in0=src[:, 16:16 + S], in1=src[:, 16 - sh:16 - sh + S], op=ALU.add),
                        reads=[res[(step + 1) % 2]], writes=[res[step % 2]])
                    sch.op(eng, call("memset", dst[:, 0:16], 0.0), writes=[res[step % 2]])
                cur = dst
                cres = res[step % 2]
                sh *= 2
                step += 1
            sch.op(eng, call("tensor_tensor", out=corr_[:, :], in0=cur[:, 16:32], in1=invcnt[:, g, :], op=ALU.mult),
                   reads=[cres, "invcnt"], writes=[corr_n])
            sch.op(eng, call("tensor_tensor", out=pooled[:, c, 0:16], in0=corr_[:, :], in1=hp[:, 16:32], op=ALU.subtract),
                   reads=[corr_n, f"hTc{c}"], writes=[f"hTc{c}"])
            sch.op("dve", call("scalar_tensor_tensor", out=pooled[:, c, 16:S], in0=cur[:, 32:16 + S], scalar=1.0 / w,
                               in1=hp[:, 32:16 + S], op0=ALU.mult, op1=ALU.subtract),
                   reads=[cres, f"hTc{c}"], writes=[f"hTc{c}"])
        if DEBUG.get("dump"):
            pdump = reg.alloc([128, NCH, 32], F32, "pdump")
            sch.op("dve", call("tensor_copy", out=pdump[:, :, :], in_=hT[:, :, PADL:PADL + 32]), reads=[f"hTc{c}" for c in range(NCH)], writes=["pdump"])
            dump(f"pooled{layer}", pdump, [128, NCH, 32], "pdump")
        hs2 = reg.alloc([128, NCH, 4, 19], F32, "hs2")
        for c in range(NCH):
            w = 2 << (c // 2)
            sch.op("dve", call("tensor_tensor", out=hs2[:, c, :, 15:19], in0=hse[:, c, :, 15:19],
                                                        in1=hse[:, c, :, 14:18], op=ALU.add),
                   reads=["hse"], writes=["hs2"])
            for k in range(2, w):
                sch.op("dve", call("tensor_tensor", out=hs2[:, c, :, 15:19], in0=hs2[:, c, :, 15:19],
                                                                 in1=hse[:, c, :, 15 - k:19 - k], op=ALU.add),
                       reads=["hse", "hs2"], writes=["hs2"])
            sch.op("dve", call("scalar_tensor_tensor",
                out=pooled[:, c, S:S + NS].rearrange("p (b t) -> p b t", t=4), in0=hs2[:, c, :, 15:19],
                scalar=1.0 / w, in1=hse[:, c, :, 15:19], op0=ALU.mult, op1=ALU.subtract),
                reads=["hs2", "hse"], writes=[f"hTc{c}"])
        wi, wt = load_w(pool_w[j].rearrange("g (k p) d -> p (g k) d", p=128), [128, 8, 256])
        mst = reg.alloc([128, NCH, 528], F32, "mst")
        for ti, segs in enumerate(TILES):
            rs = rstd[ti % 2]
            o = 0
            for si, (c0, n) in enumerate(segs):
                for dc in range(NCH):
                    g = dc // 2
                    b = next_bank()
                    for k in range(2):
                        sch.op("pe", call("matmul",
                            PS[b][:, 0:n], lhsT=wt[:, 2 * g + k, (dc % 2) * 128:(dc % 2 + 1) * 128],
                            rhs=pooled[:, 2 * g + k, c0:c0 + n], start=(k == 0), stop=(k == 1)),
                            reads=[f"wr{wi}", f"hTc{2 * g + k}"], writes=[f"ps{b}"], signal=(k == 1))
                    sch.op("act", call("activation",
                        out=mst[:, dc, o:o + n], in_=PS[b][:, 0:n], func=AF.Copy, scale=gvec(V_PSCALE + j, dc)),
                        reads=[f"ps{b}", "vecs"], writes=["mst"])
                o += n
            postnorm_residual(layer * 4 + 1, segs, mst, sq, rs)

    def conv_layer(layer):
        reg.reset()
        PADU = 32
        uT = reg.alloc([128, NCH, PADU + S], BF16, "uT")
        use = reg.alloc([128, NCH, 4, 34], BF16, "use")
        ufp = reg.alloc([128, NCH, 48], F32, "ufp")
        sq = reg.alloc([128, NCH, 528], BF16, "sq")
        mk = reg.mark()
        hT = reg.alloc([128, NCH, T], BF16, "hT")
        sg = [reg.alloc([128, 512], F32, f"sg{i}") for i in range(2)]
        sst = reg.alloc([128, D], F32, "sst")
        sch.op("pool", call("memset", uT[:, :, 0:PADU], 0.0), writes=["uT"])
        for ti, segs in enumerate(TILES):
            rs = rstd[ti % 2]
            hviews = [[hT[:, c, c0:c0 + n] for (c0, n) in segs] for c in range(NCH)]
            prenorm(layer * 4 + 0, segs, sq, rs, lambda c, si, hv=hviews: hv[c][si], "hT")
        sch.dma("sp", call("dma_start", out=sst[0:120, :], in_=st_conv.rearrange("b r d -> (b r) d")), writes=["sst"])
        for c in range(NCH):
            b = next_bank()
            sch.op("pe", call("transpose", out=PS[b][:, 0:120], in_=sst[0:120, c * 128:(c + 1) * 128],
                              identity=ident_f[0:120, 0:120]), reads=["sst", "ident_f"], writes=[f"ps{b}"])
            sch.op("act", call("copy", out=use[:, c, :, 0:30], in_=PS[b][:, 0:120].rearrange("p (b r) -> p b r", r=30)),
                   reads=[f"ps{b}"], writes=["use"])
        sch.dma("sp", call("dma_start", out=o_conv_s[:, 0:26, :], in_=st_conv[:, 4:30, :]), writes=["o_conv_s"])
        w1v = w_pw1.rearrange("(k p) f -> p k f", p=128)
        wts = [load_w(w1v[:, :, q * 512:(q + 1) * 512], [128, NCH, 512]) for q in range(4)]

        def glu_tile(c0, n, u_out_fn, ufp_fn):
            for c in range(NCH):
                ba = next_bank()
                bg = next_bank()
                for which, bb, cc in ((0, ba, c), (1, bg, c + 8)):
                    wi, wt = wts[cc // 4]
                    for k in range(NCH):
                        sch.op("pe", call("matmul", PS[bb][:, 0:n], lhsT=wt[:, k, (cc % 4) * 128:(cc % 4 + 1) * 128],
                                          rhs=hT[:, k, c0:c0 + n], start=(k == 0), stop=(k == NCH - 1)),
                               reads=[f"wr{wi}", "hT"], writes=[f"ps{bb}"], signal=(k == NCH - 1))
                g_ = sg[c % 2]
                sch.op("act", call("activation", out=g_[:, 0:n], in_=PS[bg][:, 0:n], func=AF.Sigmoid,
                                   bias=gvec(V_B1 + 1, c), scale=1.0), reads=[f"ps{bg}", "vecs"], writes=[g_.name])
                sch.op("dve", call("scalar_tensor_tensor", out=u_out_fn(c), in0=PS[ba][:, 0:n], scalar=gvec(V_B1, c),
                                   in1=g_[:, 0:n], op0=ALU.add, op1=ALU.mult),
                       reads=[f"ps{ba}", g_.name, "vecs"], writes=["uT"])
                if ufp_fn is not None:
                    lo, cnt, dst = ufp_fn(c)
                    sch.op("dve", call("scalar_tensor_tensor", out=dst, in0=PS[ba][:, lo:lo + cnt], scalar=gvec(V_B1, c),
                                       in1=g_[:, lo:lo + cnt], op0=ALU.add, op1=ALU.mult),
                           reads=[f"ps{ba}", g_.name, "vecs"], writes=["ufp"])

        for ti in range(4):
            c0 = ti * 512
            glu_tile(c0, 512, lambda c, c0=c0: uT[:, c, PADU + c0:PADU + c0 + 512],
                     (lambda c: (480, 32, ufp[:, c, 0:32])) if ti == 3 else None)
        glu_tile(S, NS, lambda c: use[:, c, :, 30:34], lambda c: (0, NS, ufp[:, c, 32:48]))
        ost = reg.alloc([128, D], F32, "ost")
        for (lo, cnt, dst_aps) in ((2, 30, [(o_conv_p[:, :], 0, 30)]),
                                   (32, 16, [(o_conv_s[bq, 26:30, :], 4 * bq, 4) for bq in range(4)])):
            b = next_bank()
            b2 = next_bank()
            for c in range(NCH):
                bb = b if c < 4 else b2
                sch.op("pe", call("transpose", out=PS[bb][0:cnt, (c % 4) * 128:(c % 4 + 1) * 128],
                                  in_=ufp[:, c, lo:lo + cnt], identity=ident_f[:, :]),
                       reads=["ufp", "ident_f"], writes=[f"ps{bb}"], signal=(c % 4 == 3))
            sch.op("dve", call("tensor_copy", out=ost[0:cnt, 0:512], in_=PS[b][0:cnt, :]), reads=[f"ps{b}"], writes=["ost"])
            sch.op("dve", call("tensor_copy", out=ost[0:cnt, 512:1024], in_=PS[b2][0:cnt, :]), reads=[f"ps{b2}"], writes=["ost"])
            for (dap, r0, nr) in dst_aps:
                sch.dma("sp", call("dma_start", out=dap, in_=ost[r0:r0 + nr, :]), reads=["ost"], writes=["o_conv"])
        reg.release(mk)
        Dc = [reg.alloc([128, 31, 128], BF16, f"Dc{i}") for i in range(2)]
        mst = reg.alloc([128, NCH, 528], F32, "mst")
        ybf = reg.alloc([128, NCH, 528], BF16, "ybf")
        zin = reg.alloc([128, NCH, 528], BF16, "zin")
        mu = reg.alloc([128, 528], F32, "mu")
        tmp = [reg.alloc([128, 528], F32, f"ctmp{i}") for i in range(3)]
        w2v = w_pw2.rearrange("(k p) f -> p k f", p=128)
        w2 = [load_w(w2v[:, :, q * 512:(q + 1) * 512], [128, NCH, 512]) for q in range(2)]
        dcn = [0]

        def build_diag(seq):
            c = seq % NCH
            dt_ = Dc[seq % 2]
            for jt in range(31):
                if jt % 2 == 0:
                    sch.op("act", call("activation", out=dt_[:, jt, :], in_=ident_b[:, :], func=AF.Copy,
                                       scale=vecs[:, c, V_WDW + jt:V_WDW + jt + 1]),
                           reads=["ident_b", "vecs"], writes=[dt_.name + "a"])
                else:
                    sch.op("dve", call("tensor_scalar", out=dt_[:, jt, :], in0=ident_b[:, :],
                                       scalar1=vecs[:, c, V_WDW + jt:V_WDW + jt + 1], scalar2=None, op0=ALU.mult),
                           reads=["ident_b", "vecs"], writes=[dt_.name + "d"])

        NSEQ = len(TILES) * NCH
        build_diag(0)
        for ti, segs in enumerate(TILES):
            rs = rstd[ti % 2]
            offs = [0, 512]
            for c in range(NCH):
                seq = dcn[0]
                dt_ = Dc[seq % 2]
                dcn[0] += 1
                if seq + 1 < NSEQ:
                    build_diag(seq + 1)
                for si, (c0, n) in enumerate(segs):
                    b = next_bank()
                    for jt in range(31):
                        if si == 0:
                            rhs = uT[:, c, PADU - 30 + c0 + jt:PADU - 30 + c0 + jt + n]
                            outp = PS[b][:, 0:n]
                        else:
                            rhs = use[:, c, :, jt:jt + 4]
                            outp = PS[b][:, 0:NS].rearrange("p (b t) -> p b t", t=4)
                        sch.op("pe", call("matmul", outp, lhsT=dt_[:, jt, :], rhs=rhs, start=(jt == 0), stop=(jt == 30)),
                               reads=[dt_.name + "a", dt_.name + "d", "uT", "use"], writes=[f"ps{b}"], signal=(jt == 30))
                    o = offs[si]
                    sch.op("act", call("activation", out=mst[:, c, o:o + n], in_=PS[b][:, 0:n], func=AF.Identity,
                                       bias=gvec(V_BDW, c), scale=1.0), reads=[f"ps{b}", "vecs"], writes=["mst"])
                    sch.op("act", call("activation", out=ybf[:, c, o:o + n], in_=PS[b][:, 0:n], func=AF.Identity,
                                       bias=gvec(V_BDW, c), scale=1.0), reads=[f"ps{b}", "vecs"], writes=["ybf"])
                    sch.op("dve", call("tensor_tensor", out=sq[:, c, o:o + n], in0=mst[:, c, o:o + n],
                                       in1=mst[:, c, o:o + n], op=ALU.mult), reads=["mst"], writes=[sq.name + str(c)])
            for si, (c0, n) in enumerate(segs):
                o = offs[si]
                bm = next_bank()
                bv = next_bank()
                for c in range(NCH):
                    sch.op("pe", call("matmul", PS[bm][:, 0:n], lhsT=ones_m[:, :], rhs=ybf[:, c, o:o + n],
                                      start=(c == 0), stop=(c == NCH - 1)),
                           reads=["ybf", "ones_m"], writes=[f"ps{bm}"], signal=(c == NCH - 1))
                for c in range(NCH):
                    sch.op("pe", call("matmul", PS[bv][:, 0:n], lhsT=ones_m[:, :], rhs=sq[:, c, o:o + n],
                                      start=(c == 0), stop=(c == NCH - 1)),
                           reads=[sq.name + str(c), "ones_m"], writes=[f"ps{bv}"], signal=(c == NCH - 1))
                sch.op("act", call("copy", out=mu[:, o:o + n], in_=PS[bm][:, 0:n]), reads=[f"ps{bm}"], writes=["mu"])
                t0_ = tmp[0]
                sch.op("dve", call("tensor_tensor", out=t0_[:, o:o + n], in0=mu[:, o:o + n], in1=mu[:, o:o + n], op=ALU.mult),
                       reads=["mu"], writes=[t0_.name])
                sch.op("dve", call("tensor_tensor", out=t0_[:, o:o + n], in0=PS[bv][:, 0:n], in1=t0_[:, o:o + n],
                                   op=ALU.subtract), reads=[f"ps{bv}", t0_.name], writes=[t0_.name])
                sch.op("act", call("activation", out=rs[:, o:o + n], in_=t0_[:, o:o + n], func=AF.Sqrt,
                                   bias=epsc[:, 0:1], scale=1.0), reads=[t0_.name, "epsc"], writes=[rs.name])
                sch.op("dve", call("reciprocal", out=rs[:, o:o + n], in_=rs[:, o:o + n]), reads=[rs.name], writes=[rs.name])
                for c in range(NCH):
                    t1 = tmp[1 + c % 2]
                    sch.op("dve", call("tensor_tensor", out=t1[:, o:o + n], in0=mst[:, c, o:o + n], in1=mu[:, o:o + n],
                                       op=ALU.subtract), reads=["mst", "mu"], writes=[t1.name])
                    sch.op("dve", call("tensor_tensor", out=t1[:, o:o + n], in0=t1[:, o:o + n], in1=rs[:, o:o + n],
                                       op=ALU.mult), reads=[t1.name, rs.name], writes=[t1.name])
                    sch.op("act", call("activation", out=zin[:, c, o:o + n], in_=t1[:, o:o + n], func=AF.Silu,
                                       bias=gvec(V_LNB, c), scale=gvec(V_LNG, c)),
                           reads=[t1.name, "vecs"], writes=["zin"])
            for si, (c0, n) in enumerate(segs):
                o = offs[si]
                for dc in range(NCH):
                    wi, wt = w2[dc // 4]
                    b = next_bank()
                    for k in range(NCH):
                        sch.op("pe", call("matmul", PS[b][:, 0:n], lhsT=wt[:, k, (dc % 4) * 128:(dc % 4 + 1) * 128],
                                          rhs=zin[:, k, o:o + n], start=(k == 0), stop=(k == NCH - 1)),
                               reads=[f"wr{wi}", "zin"], writes=[f"ps{b}"], signal=(k == NCH - 1))
                    sch.op("act", call("activation", out=mst[:, dc, o:o + n], in_=PS[b][:, 0:n], func=AF.Identity,
                                       bias=gvec(V_B2, dc), scale=1.0), reads=[f"ps{b}", "vecs"], writes=["mst"])
            postnorm_residual(layer * 4 + 1, segs, mst, sq, rs)

    def attn_layer(layer):
        reg.reset()
        hT = reg.alloc([128, NCH, T], BF16, "hT")
        oT = reg.alloc([128, NCH, T], BF16, "oT")
        sq = reg.alloc([128, NCH, 528], BF16, "sq")
        m0 = reg.mark()
        rb = reg.alloc([32, 48], F32, "rb")
        expb = reg.alloc([32, 48], F32, "expb")
        ohf = reg.alloc([32, 3, 384], F32, "ohf")
        fsb = reg.alloc([16, 3, 384], BF16, "fsb")
        sch.dma("sp", call("dma_start", out=rb[:, :], in_=rel_bias_d[:, :]), writes=["rb"])
        sch.dma("sp", call("dma_start", out=ohf[:, :, :], in_=ohf_d[:, :, :]), writes=["ohf"])
        rbb = reg.alloc([32, 48], BF16, "rbb")
        ohfb = reg.alloc([32, 3, 384], BF16, "ohfb")
        sch.op("act", call("activation", out=expb[:, :], in_=rb[:, :], func=AF.Exp), reads=["rb"], writes=["expb"])
        sch.op("dve", call("tensor_copy", out=rbb[:, :], in_=expb[:, :]), reads=["expb"], writes=["rbb"])
        sch.op("dve", call("tensor_copy", out=ohfb[:, :, :], in_=ohf[:, :, :]), reads=["ohf"], writes=["ohfb"])
        for g in range(3):
            b = next_bank()
            sch.op("pe", call("matmul", PS[b][0:16, 0:384], lhsT=rbb[:, g * 16:(g + 1) * 16], rhs=ohfb[:, g, :],
                              start=True, stop=True), reads=["rbb", "ohfb"], writes=[f"ps{b}"])
            sch.op("dve", call("tensor_copy", out=fsb[:, g, :], in_=PS[b][0:16, 0:384]), reads=[f"ps{b}"], writes=["fsb"])
            sch.dma("sp", call("dma_start", out=escr[g, :, :, :],
                               in_=fsb[:, g, :].unsqueeze(1).broadcast_to([16, 128, 384])),
                    reads=["fsb"], writes=["escr"])
        for ti, segs in enumerate(TILES):
            rs = rstd[ti % 2]
            hviews = [[hT[:, c, c0:c0 + n] for (c0, n) in segs] for c in range(NCH)]
            prenorm(layer * 4 + 0, segs, sq, rs, lambda c, si, hv=hviews: hv[c][si], "hT")
        sch.dma("sp", call("dma_start", out=xt_scr[:, :], in_=XT[:, :, :].rearrange("p c t -> p (c t)")),
                reads=["XT"], writes=["xt_scr"])
        sch.barrier()
        copy_chunks = []
        for g in range(3):
            win = WINS[g]
            rows = win - 4
            nchunk = (rows + 255) // 256
            for bq in range(4):
                for ch in range(nchunk):
                    r0 = rows * ch // nchunk
                    r1 = rows * (ch + 1) // nchunk
                    copy_chunks.append((g, bq, r0, r1))
        copy_chunks.sort(key=lambda x: -x[0])

        def issue_copies(n):
            for _ in range(n):
                if not copy_chunks:
                    return
                g, bq, r0, r1 = copy_chunks.pop(0)
                sch.dma("act", call("dma_start", out=o_kv_s[g][bq, r0:r1, :], in_=cache_kv[g][bq, 4 + r0:4 + r1, :]),
                        writes=[f"o_kv_s{g}"])
        rx = Region(sb, 0, nbytes([128, NCH, T], F32))
        reg.cur = m0
        mk = reg.mark()
        ones64 = reg.alloc([128, 128], BF16, "ones64")
        sch.op("pool", call("memset", ones64[:, :], 1.0), writes=["ones64"])
        QB = [rx.alloc([128, S], BF16, f"qb{i}") for i in range(2)]
        KM = [[rx.alloc([128, S], BF16, f"km{i}{h}") for h in range(2)] for i in range(2)]
        for i in range(2):
            sch.op("pool", call("memset", KM[i][0][64:128, :], 0.0), writes=[f"km{i}0"])
            sch.op("pool", call("memset", KM[i][1][0:64, :], 0.0), writes=[f"km{i}1"])
        qk_n = [0]
        OL = rx.alloc([128, 2, S], F32, "OL")
        Vb = [reg.alloc([128, 16, 128], BF16, f"vb{g}") for g in range(3)]
        Ep = [reg.alloc([128, 3, 2, 256], BF16, f"ep{i}") for i in range(2)]
        Pt = [reg.alloc([128, 512], BF16, f"pt{i}") for i in range(4)]
        kvst = [rx.alloc([128, 16, 128], F32, f"kvst{i}") for i in range(2)]
        qst = reg.alloc([NS, 9, 128], F32, "qst")
        NW = 16
        wq = [sb.at(wring_off + i * 2048, [128, NCH, 128], BF16) for i in range(NW)]
        wq_rr = [0]
        wv = w_qkv.rearrange("(k p) f -> p k f", p=128)
        kvst_n = [0]
        pt_n = [0]

        def load_wblk(col0):
            i = wq_rr[0]
            wq_rr[0] = (i + 1) % NW
            sch.dma("pool", call("dma_start", out=wq[i][:, :, :], in_=wv[:, :, col0:col0 + 128]), writes=[f"wq{i}"])
            return i, wq[i]

        NPAIR = DEBUG.get("attn_pairs", NCH)
        wnext = None
        for j in range(NPAIR):
            ep = Ep[j % 2]
            PARTS = DEBUG.get("attn_parts", "ewsft")
            for g in range(3):
                if "e" not in PARTS:
                    break
                src = bass.AP(escr, ((g * 16 + 2 * j) * 128 * 384) + 128, [[383, 128], [128 * 384, 2], [1, 256]])
                sch.dma("sp", call("dma_start", out=ep[:, g, :, :], in_=src), reads=["escr"], writes=[ep.name])
            order = [(g, w) for g in range(3) for w in range(3)]
            NPF = NW - 9
            if j == 0:
                wnext = {}
            wblk = wnext
            for (g, w) in order:
                if (g, w) not in wblk:
                    wblk[(g, w)] = load_wblk((3 * g + w) * D + j * 128)
            wnext = {}
            if j + 1 < NPAIR:
                for (g, w) in order[:NPF]:
                    wnext[(g, w)] = load_wblk((3 * g + w) * D + (j + 1) * 128)
            for g in range(3):
                if "s" not in PARTS:
                    break
                b = next_bank()
                for w in range(3):
                    wi, wt = wblk[(g, w)]
                    for k in range(NCH):
                        sch.op("pe", call("matmul", PS[b][0:NS, w * 128:(w + 1) * 128], lhsT=hT[:, k, S:S + NS], rhs=wt[:, k, :],
                                          start=(k == 0), stop=(k == NCH - 1)),
                               reads=[f"wq{wi}", "hT"], writes=[f"ps{b}"], signal=(k == NCH - 1))
                sch.op("act", call("copy", out=qst[:, 3 * g:3 * g + 3, :],
                                   in_=PS[b][0:NS, 0:384].rearrange("p (w c) -> p w c", c=128)),
                       reads=[f"ps{b}"], writes=["qst"])
            dst = bass.AP(qkvs_scr, j * 128, [[9 * D, NS], [D, 9], [1, 128]])
            if "s" in PARTS:
                sch.dma("sp", call("dma_start", out=dst, in_=qst[:, :, :]), reads=["qst"], writes=["qkvs_scr"])
            for g in range(3):
                dil = DILS[g]
                L = S // dil
                nblk = L // 128
                issue_copies(2)
                qi = qk_n[0] % 2
                qk_n[0] += 1
                for w in range(2):
                    if "f" not in PARTS:
                        break
                    wi, wt = wblk[(g, w)]
                    for tt in range(4):
                        b = next_bank()
                        for k in range(NCH):
                            sch.op("pe", call("matmul", PS[b][:, 0:512], lhsT=wt[:, k, :], rhs=hT[:, k, tt * 512:(tt + 1) * 512],
                                              start=(k == 0), stop=(k == NCH - 1)),
                                   reads=[f"wq{wi}", "hT"], writes=[f"ps{b}"], signal=(k == NCH - 1))
                        lw = 512 // dil
                        if w == 0:
                            dstv = QB[qi][:, :].rearrange("p (r l) -> p r l", r=dil)
                            sch.op("dve", call("tensor_copy", out=dstv[:, :, tt * lw:(tt + 1) * lw],
                                               in_=PS[b][:, 0:512].rearrange("p (l r) -> p r l", r=dil)),
                                   reads=[f"ps{b}"], writes=[f"qb{qi}"])
                        else:
                            for h in range(2):
                                rows = slice(h * 64, (h + 1) * 64)
                                dstv = KM[qi][h][rows, :].rearrange("p (r l) -> p r l", r=dil)
                                sch.op("act", call("copy", out=dstv[:, :, tt * lw:(tt + 1) * lw],
                                                   in_=PS[b][rows, 0:512].rearrange("p (l r) -> p r l", r=dil)),
                                       reads=[f"ps{b}"], writes=[f"km{qi}{h}"])
                for w in (2, 1):
                    if "t" not in PARTS:
                        break
                    wi, wt = wblk[(g, w)]
                    if w == 2:
                        tiles = list(range(16))
                    elif g == 0:
                        tiles = [15]
                    elif g == 1:
                        tiles = [3, 7, 11, 15]
                    else:
                        tiles = list(range(16))
                    st = kvst[kvst_n[0] % 2]
                    kvst_n[0] += 1
                    for q0 in range(0, len(tiles), 4):
                        grp = tiles[q0:q0 + 4]
                        b = next_bank()
                        for gi, n in enumerate(grp):
                            r, blk = divmod(n, nblk)
                            t0 = r + dil * 128 * blk
                            for k in range(NCH):
                                sch.op("pe", call("matmul", PS[b][:, gi * 128:(gi + 1) * 128],
                                                  lhsT=hT[:, k, t0:t0 + dil * 127 + 1:dil], rhs=wt[:, k, :],
                                                  start=(k == 0), stop=(k == NCH - 1)),
                                       reads=[f"wq{wi}", "hT"], writes=[f"ps{b}"],
                                       signal=(k == NCH - 1 and gi == len(grp) - 1))
                        ng = len(grp)
                        contiguous = all(grp[i] == grp[0] + i for i in range(ng))
                        if contiguous:
                            sch.op("act", call("copy", out=st[:, grp[0]:grp[0] + ng, :],
                                               in_=PS[b][:, 0:ng * 128].rearrange("p (n c) -> p n c", c=128)),
                                   reads=[f"ps{b}"], writes=[st.name])
                            if w == 2:
                                sch.op("dve", call("tensor_copy", out=Vb[g][:, grp[0]:grp[0] + ng, :],
                                                   in_=st[:, grp[0]:grp[0] + ng, :]),
                                       reads=[st.name], writes=[f"vb{g}"])
                        else:
                            for gi, n in enumerate(grp):
                                sch.op("act", call("copy", out=st[:, n, :], in_=PS[b][:, gi * 128:(gi + 1) * 128]),
                                       reads=[f"ps{b}"], writes=[st.name])
                    col = (w - 1) * D + j * 128
                    if g == 0:
                        dsts = [(bass.AP(o_kv_p[0], col, [[RS, 128], [1, 128]]), st[:, 15, :])]
                    elif g == 1:
                        dsts = [(bass.AP(o_kv_p[1], col, [[4 * RS, 128], [RS, 4], [1, 128]]), st[:, 3:16:4, :])]
                    else:
                        dsts = [(bass.AP(o_kv_p[2], col, [[16 * RS, 128], [RS, 16], [1, 128]]), st[:, :, :])]
                    for (dap, sap) in dsts:
                        sch.dma("sp", call("dma_start", out=dap, in_=sap), reads=[st.name], writes=[f"o_kv_p{g}"])
                blocks = [(r, qb) for r in range(dil) for qb in range(nblk)]

                def colof(r, blk):
                    return (r * nblk + blk) * 128

                def emit_qk(r, qb):
                    bS = next_bank()
                    p_ = Pt[pt_n[0] % 4]
                    pt_n[0] += 1
                    nparts = 2 if qb > 0 else 1
                    cq = colof(r, qb)
                    for h in range(2):
                        rows = slice(h * 64, (h + 1) * 64)
                        for part in range(nparts):
                            ck = colof(r, qb - part)
                            last = (h == 1 and part == nparts - 1)
                            sch.op("pe", call("matmul", PS[bS][:, h * 256 + part * 128:h * 256 + part * 128 + 128],
                                              lhsT=KM[qi][h][:, ck:ck + 128], rhs=QB[qi][:, cq:cq + 128],
                                              start=True, stop=True),
                                   reads=[f"qb{qi}", f"km{qi}{h}"], writes=[f"ps{bS}"], signal=last)
                    if nparts == 2:
                        sch.op("act", call("activation", out=p_[:, :], in_=PS[bS][:, :], func=AF.Exp, scale=0.125),
                               reads=[f"ps{bS}"], writes=[p_.name])
                        sch.op("dve", call("tensor_tensor", out=p_[:, :], in0=p_[:, :],
                                           in1=ep[:, g, :, :].rearrange("p h q -> p (h q)"), op=ALU.mult),
                               reads=[p_.name, ep.name], writes=[p_.name])
                    else:
                        pv = p_[:, :].rearrange("p (h q) -> p h q", h=2)[:, :, 0:128]
                        sch.op("act", call("activation", out=pv, in_=PS[bS][:, :].rearrange("p (h q) -> p h q", h=2)[:, :, 0:128],
                                           func=AF.Exp, scale=0.125), reads=[f"ps{bS}"], writes=[p_.name])
                        sch.op("dve", call("tensor_tensor", out=pv, in0=pv, in1=ep[:, g, :, 0:128], op=ALU.mult),
                               reads=[p_.name, ep.name], writes=[p_.name])
                    return p_, nparts

                def emit_pv(r, qb, p_, nparts):
                    if DEBUG.get("attn_nopv"):
                        return
                    bO = next_bank()
                    pview = p_[:, :].rearrange("p (h x q) -> p h x q", h=2, x=2)
                    for which in range(2):
                        for pi, part in enumerate(range(nparts - 1, -1, -1)):
                            n = r * nblk + qb - part
                            lhsT = Vb[g][:, n, :] if which == 0 else ones64[:, :]
                            last = (which == 1 and pi == nparts - 1)
                            sch.op("pe", call("matmul", PS[bO][:, which * 256:(which + 1) * 256].rearrange("p (h q) -> p h q", h=2),
                                              lhsT=lhsT, rhs=pview[:, :, part, :], start=(pi == 0), stop=(pi == nparts - 1)),
                                   reads=[p_.name, f"vb{g}", "ones64"], writes=[f"ps{bO}"], signal=last)
                    t0 = r + dil * 128 * qb
                    for h in range(2):
                        rows = slice(h * 64, (h + 1) * 64)
                        ov = OL[rows, :, t0:t0 + dil * 127 + 1:dil]
                        pin = PS[bO][rows, :].rearrange("p (w h q) -> p w h q", w=2, h=2)[:, :, h, :]
                        if g == 0:
                            sch.op("dve", call("tensor_copy", out=ov, in_=pin), reads=[f"ps{bO}"], writes=["OL"])
                        else:
                            sch.op("dve", call("tensor_tensor", out=ov, in0=ov, in1=pin, op=ALU.add),
                                   reads=[f"ps{bO}", "OL"], writes=["OL"])

                prev = None
                if DEBUG.get("attn_nocore"):
                    blocks = []
                    prev = "skip"
                pend = []
                for (r, qb) in blocks:
                    pend.append((r, qb) + emit_qk(r, qb))
                    if len(pend) > 2:
                        emit_pv(*pend.pop(0))
                while pend:
                    emit_pv(*pend.pop(0))
            sch.op("act", call("activation", out=OL[:, 1, :], in_=OL[:, 1, :], func=AF.Ln), reads=["OL"], writes=["OL"])
            sch.op("act", call("activation", out=OL[:, 1, :], in_=OL[:, 1, :], func=AF.Exp, scale=-1.0),
                   reads=["OL"], writes=["OL"])
            sch.op("dve", call("tensor_tensor", out=oT[:, j, 0:S], in0=OL[:, 0, :], in1=OL[:, 1, :], op=ALU.mult),
                   reads=["OL"], writes=["oT"])
        issue_copies(len(copy_chunks))
        sch.barrier()
        rx.cur = 0
        reg.cur = mk
        for g in range(3):
            win = WINS[g]
            for bq in range(4):
                sch.dma("sp", call("dma_start", out=o_kv_s[g][bq, win - 4:win, :],
                                   in_=qkvs_scr[4 * bq:4 * bq + 4, g * 3 * D + D:g * 3 * D + 3 * D]),
                        reads=["qkvs_scr"], writes=[f"o_kv_s{g}"])
        rb2 = reg.alloc([32, 48], F32, "rb2")
        rbb2 = reg.alloc([32, 48], BF16, "rbb2")
        ohm_f = rx.alloc([32, 3, 4, 128], F32, "ohm_f")
        ohm = reg.alloc([32, 3, 4, 128], BF16, "ohm")
        ohx_f = rx.alloc([32, 3, 4, NS], F32, "ohx_f")
        ohx = reg.alloc([32, 3, 4, NS], BF16, "ohx")
        selc_f = rx.alloc([128, NS, NS], F32, "selc_f")
        selc = reg.alloc([128, NS, NS], BF16, "selc")
        Em = reg.alloc([128, 3, 4, 16], F32, "Em")
        Ex = reg.alloc([NS, 3, 4, 16], F32, "Ex")
        sch.dma("sp", call("dma_start", out=rb2[:, :], in_=rel_bias_d[:, :]), writes=["rb2"])
        sch.dma("sp", call("dma_start", out=ohm_f[:, :, :, :], in_=ohm_d[:, :, :, :]), writes=["ohm_f"])
        sch.dma("sp", call("dma_start", out=ohx_f[:, :, :, :], in_=ohx_d[:, :, :, :]), writes=["ohx_f"])
        sch.dma("sp", call("dma_start", out=selc_f[:, :, :], in_=selc_d[:, :, :]), writes=["selc_f"])
        sch.op("act", call("activation", out=rb2[:, :], in_=rb2[:, :], func=AF.Exp), reads=["rb2"], writes=["rb2"])
        sch.op("dve", call("tensor_copy", out=rbb2[:, :], in_=rb2[:, :]), reads=["rb2"], writes=["rbb2"])
        sch.op("dve", call("tensor_copy", out=ohm[:, :, :, :].rearrange("p a b c -> p (a b c)"),
                           in_=ohm_f[:, :, :, :].rearrange("p a b c -> p (a b c)")), reads=["ohm_f"], writes=["ohm"])
        sch.op("dve", call("tensor_copy", out=ohx[:, :, :, :].rearrange("p a b c -> p (a b c)"),
                           in_=ohx_f[:, :, :, :].rearrange("p a b c -> p (a b c)")), reads=["ohx_f"], writes=["ohx"])
        sch.op("dve", call("tensor_copy", out=selc[:, :, :].rearrange("p a b -> p (a b)"),
                           in_=selc_f[:, :, :].rearrange("p a b -> p (a b)")), reads=["selc_f"], writes=["selc"])
        for g in range(3):
            b = next_bank()
            for t in range(4):
                sch.op("pe", call("matmul", PS[b][:, t * 16:(t + 1) * 16], lhsT=ohm[:, g, t, :], rhs=rbb2[:, g * 16:(g + 1) * 16],
                                  start=True, stop=True), reads=["ohm", "rbb2"], writes=[f"ps{b}"], signal=(t == 3))
            sch.op("dve", call("tensor_copy", out=Em[:, g, :, :].rearrange("p t h -> p (t h)"), in_=PS[b][:, 0:64]),
                   reads=[f"ps{b}"], writes=["Em"])
            b = next_bank()
            for r in range(4):
                sch.op("pe", call("matmul", PS[b][0:NS, r * 16:(r + 1) * 16], lhsT=ohx[:, g, r, :], rhs=rbb2[:, g * 16:(g + 1) * 16],
                                  start=True, stop=True), reads=["ohx", "rbb2"], writes=[f"ps{b}"], signal=(r == 3))
            sch.op("dve", call("tensor_copy", out=Ex[:, g, :, :].rearrange("p t h -> p (t h)"), in_=PS[b][0:NS, 0:64]),
                   reads=[f"ps{b}"], writes=["Ex"])
        bO1 = next_bank()
        bO2 = next_bank()
        bL = next_bank()
        ps_reserved.update([bO1, bO2, bL])
        NRING = 5
        KVt = [rx.alloc([128, RS], BF16, f"kvt{i}") for i in range(NRING)]
        qbt = [rx.alloc([128, D], BF16, f"qbt{i}") for i in range(NRING)]
        prod = [reg.alloc([128, D], F32, f"prod{i}") for i in range(2)]
        Zb = [reg.alloc([128, D], BF16, f"zb{i}") for i in range(2)]
        Ssc = [reg.alloc([128, 16], F32, f"Ssc{i}") for i in range(2)]
        Pm = [reg.alloc([128, 16], F32, f"Pm{i}") for i in range(2)]
        Pmb = [reg.alloc([128, 16], BF16, f"Pmb{i}") for i in range(2)]
        first = [True]
        AX = mybir.AxisListType.X

        def accum(lhsT, z, pb, res_reads, is_last):
            st_ = first[0]
            first[0] = False
            sch.op("pe", call("matmul", PS[bO1][0:NS, 0:512], lhsT=lhsT, rhs=z[:, 0:512], start=st_, stop=is_last),
                   reads=res_reads, writes=[f"ps{bO1}"], signal=False)
            sch.op("pe", call("matmul", PS[bO2][0:NS, 0:512], lhsT=lhsT, rhs=z[:, 512:1024], start=st_, stop=is_last),
                   reads=res_reads, writes=[f"ps{bO2}"], signal=False)
            sch.op("pe", call("matmul", PS[bL][0:NS, 0:16], lhsT=lhsT, rhs=pb, start=st_, stop=is_last),
                   reads=res_reads, writes=[f"ps{bL}"], signal=True)

        qall = reg.alloc([NS, 3, D], BF16, "qall")
        for g in range(3):
            sch.dma("pool", call("dma_start", out=qall[:, g, :], in_=qkvs_scr[:, g * 3 * D:g * 3 * D + D]),
                    reads=["qkvs_scr"], writes=["qall"])
        selr = reg.alloc([NS, NS, 128], BF16, "selr")
        sch.op("dve", call("tensor_copy", out=selr[:, :, :],
                           in_=ident_b[0:NS, 0:NS].unsqueeze(2).broadcast_to([NS, NS, 128])),
               reads=["ident_b"], writes=["selr"])
        items = [(g, bq, t) for g in range(3) for bq in range(4) for t in range(4)]
        loaded = [0]

        def issue_loads(upto):
            while loaded[0] < min(upto, len(items)):
                g, bq, t = items[loaded[0]]
                kv = KVt[loaded[0] % NRING]
                qq = qbt[loaded[0] % NRING]
                loaded[0] += 1
                dil = DILS[g]
                win = WINS[g]
                base = bq * win * RS
                if g == 0:
                    src = bass.AP(o_kv_s[0], base, [[RS, 128], [1, RS]])
                else:
                    src = bass.AP(o_kv_s[g], base + (win - 4 - dil * 127 + t) * RS, [[dil * RS, 128], [1, RS]])
                sch.dma("pool", call("dma_start", out=kv[:, :], in_=src), reads=[f"o_kv_s{g}"], writes=[kv.name])
                for hh in range(2):
                    bq_ = next_bank()
                    sch.op("pe", call("matmul", PS[bq_][:, 0:512], lhsT=selr[:, 4 * bq + t, :], rhs=qall[:, g, hh * 512:(hh + 1) * 512],
                                      start=True, stop=True), reads=["selr", "qall"], writes=[f"ps{bq_}"])
                    sch.op("act", call("copy", out=qq[:, hh * 512:(hh + 1) * 512], in_=PS[bq_][:, 0:512]),
                           reads=[f"ps{bq_}"], writes=[qq.name])

        for idx, (g, bq, t) in enumerate(items):
            issue_loads(idx + NRING)
            kv = KVt[idx % NRING]
            qq = qbt[idx % NRING]
            pr = prod[idx % 2]
            ss = Ssc[idx % 2]
            pm = Pm[idx % 2]
            pmb = Pmb[idx % 2]
            z = Zb[idx % 2]
            sch.op("dve", call("tensor_tensor", out=pr[:, :], in0=kv[:, 0:D], in1=qq[:, :], op=ALU.mult),
                   reads=[kv.name, qq.name], writes=[pr.name])
            sch.op("dve", call("tensor_reduce", out=ss[:, :], in_=pr[:, :].rearrange("p (h c) -> p h c", c=64),
                               axis=AX, op=ALU.add), reads=[pr.name], writes=[ss.name])
            sch.op("act", call("activation", out=pm[:, :], in_=ss[:, :], func=AF.Exp, scale=0.125),
                   reads=[ss.name], writes=[pm.name])
            sch.op("dve", call("tensor_tensor", out=pm[:, :], in0=pm[:, :], in1=Em[:, g, t, :], op=ALU.mult),
                   reads=[pm.name, "Em"], writes=[pm.name])
            sch.op("act", call("copy", out=pmb[:, :], in_=pm[:, :]), reads=[pm.name], writes=[pmb.name])
            sch.op("dve", call("tensor_tensor", out=z[:, :].rearrange("p (h c) -> p h c", c=64),
                               in0=kv[:, D:2 * D].rearrange("p (h c) -> p h c", c=64),
                               in1=pm[:, :].unsqueeze(2).broadcast_to([128, 16, 64]), op=ALU.mult),
                   reads=[kv.name, pm.name], writes=[z.name])
            accum(selc[:, 4 * bq + t, :], z, pmb[:, :], [z.name, pmb.name, "selc"], False)
        Kxt = rx.alloc([NS, 4, RS], BF16, "Kxt")
        Kx = Kxt[:, :, :]
        KXN = "Kxt"
        qxt = rx.alloc([NS, D], BF16, "qxt")
        qx = qxt[:, :]
        QXN = "qxt"
        Sx = reg.alloc([NS, 4, 16], F32, "Sx")
        Px = reg.alloc([NS, 4, 16], F32, "Px")
        Pxb = reg.alloc([NS, 4, 16], BF16, "Pxb")
        for g in range(3):
            win = WINS[g]
            for bq in range(4):
                src = bass.AP(cache_kv[g], bq * win * RS, [[0, 4], [RS, 4], [1, RS]])
                sch.dma("pool", call("dma_start", out=Kxt[4 * bq:4 * bq + 4, :, :], in_=src), writes=[KXN])
            sch.dma("pool", call("dma_start", out=qx, in_=qkvs_scr[:, g * 3 * D:g * 3 * D + D]),
                    reads=["qkvs_scr"], writes=[QXN])
            for r in range(4):
                pr = prod[r % 2]
                sch.op("dve", call("tensor_tensor", out=pr[0:NS, :], in0=Kx[:, r, 0:D], in1=qx, op=ALU.mult),
                       reads=[KXN, QXN], writes=[pr.name])
                sch.op("dve", call("tensor_reduce", out=Sx[:, r, :], in_=pr[0:NS, :].rearrange("p (h c) -> p h c", c=64),
                                   axis=AX, op=ALU.add), reads=[pr.name], writes=["Sx"])
            sch.op("act", call("activation", out=Px[:, :, :].rearrange("p t h -> p (t h)"),
                               in_=Sx[:, :, :].rearrange("p t h -> p (t h)"), func=AF.Exp, scale=0.125),
                   reads=["Sx"], writes=["Px"])
            sch.op("dve", call("tensor_tensor", out=Px[:, :, :], in0=Px[:, :, :], in1=Ex[:, g, :, :], op=ALU.mult),
                   reads=["Px", "Ex"], writes=["Px"])
            sch.op("dve", call("tensor_copy", out=Pxb[:, :, :], in_=Px[:, :, :]), reads=["Px"], writes=["Pxb"])
            for r in range(4):
                z = Zb[r % 2]
                sch.op("dve", call("tensor_tensor", out=z[0:NS, :].rearrange("p (h c) -> p h c", c=64),
                                   in0=Kx[:, r, D:2 * D].rearrange("p (h c) -> p h c", c=64),
                                   in1=Px[:, r, :].unsqueeze(2).broadcast_to([NS, 16, 64]), op=ALU.mult),
                       reads=[KXN, "Px"], writes=[z.name])
                accum(ident_b[0:NS, 0:NS], z[0:NS, :], Pxb[:, r, :], [z.name, "Pxb", "ident_b"], (g == 2 and r == 3))
        os_ = rx.alloc([NS, D], F32, "os_")
        lr = reg.alloc([NS, 16], F32, "lr")
        sch.op("dve", call("reciprocal", out=lr[:, :], in_=PS[bL][0:NS, 0:16]), reads=[f"ps{bL}"], writes=["lr"])
        for hh, bb in enumerate((bO1, bO2)):
            sch.op("dve", call("tensor_tensor", out=os_[:, hh * 512:(hh + 1) * 512].rearrange("p (h c) -> p h c", c=64),
                               in0=PS[bb][0:NS, 0:512].rearrange("p (h c) -> p h c", c=64),
                               in1=lr[:, hh * 8:(hh + 1) * 8].unsqueeze(2).broadcast_to([NS, 8, 64]), op=ALU.mult),
                   reads=[f"ps{bb}", "lr"], writes=["os_"])
        ps_reserved.clear()
        b = next_bank()
        for c in range(NCH):
            sch.op("pe", call("transpose", out=PS[b][:, c * NS:(c + 1) * NS], in_=os_[:, c * 128:(c + 1) * 128],
                              identity=ident_f[0:NS, 0:NS]), reads=["os_", "ident_f"], writes=[f"ps{b}"], signal=(c == NCH - 1))
        sch.op("dve", call("tensor_copy", out=oT[:, :, S:S + NS], in_=PS[b][:, 0:NCH * NS].rearrange("p (c t) -> p c t", t=NS)),
               reads=[f"ps{b}"], writes=["oT"])
        sch.barrier()
        sch.dma("sp", call("dma_start", out=XT[:, :, :].rearrange("p c t -> p (c t)"), in_=xt_scr[:, :]),
                reads=["xt_scr"], writes=["XT"])
        reg.release(mk)
        mst = reg.alloc([128, NCH, 528], F32, "mst")
        wov = w_o.rearrange("(k p) f -> p k f", p=128)
        wo = [load_w(wov[:, :, q * 512:(q + 1) * 512], [128, NCH, 512]) for q in range(2)]
        for ti, segs in enumerate(TILES):
            rs = rstd[ti % 2]
            o = 0
            for si, (c0, n) in enumerate(segs):
                for dc in range(NCH):
                    wi, wt = wo[dc // 4]
                    b = next_bank()
                    for k in range(NCH):
                        sch.op("pe", call("matmul", PS[b][:, 0:n], lhsT=wt[:, k, (dc % 4) * 128:(dc % 4 + 1) * 128],
                                          rhs=oT[:, k, c0:c0 + n], start=(k == 0), stop=(k == NCH - 1)),
                               reads=[f"wr{wi}", "oT"], writes=[f"ps{b}"], signal=(k == NCH - 1))
                    sch.op("act", call("copy", out=mst[:, dc, o:o + n], in_=PS[b][:, 0:n]), reads=[f"ps{b}"], writes=["mst"])
                o += n
            postnorm_residual(layer * 4 + 1, segs, mst, sq, rs)

    pool_j = 0
    for layer in range(4):
        if layer not in layers:
            if layer % 3 == 0:
                pool_j += 1
            continue
        kind = layer % 3
        if kind == 0:
            if not DEBUG.get("skip_mix"):
                pool_layer(layer, pool_j)
            pool_j += 1
        elif kind == 2:
            conv_layer(layer)
        else:
            attn_layer(layer)
        if not DEBUG.get("skip_ffn"):
            ffn(layer)

    reg.reset()
    ystage = [reg.alloc([128, D], F32, f"ystage{i}") for i in range(2)]

    def store_block(dst_ap, nrows, col0, i):
        st = ystage[i % 2]
        for c0 in range(0, NCH, 4):
            b = next_bank()
            for c in range(c0, c0 + 4):
                sch.op("pe", call("transpose",
                    out=PS[b][0:nrows, (c - c0) * 128:(c - c0 + 1) * 128], in_=XT[:, c, col0:col0 + nrows],
                    identity=ident_f[:, :]),
                    reads=["XT", "ident_f"], writes=[f"ps{b}"], signal=(c == c0 + 3))
            if c0 == 0:
                sch.op("dve", call("tensor_copy", out=st[0:nrows, 0:512], in_=PS[b][0:nrows, :]),
                       reads=[f"ps{b}"], writes=[f"ystage{i % 2}"])
            else:
                sch.op("act", call("copy", out=st[0:nrows, 512:1024], in_=PS[b][0:nrows, :]),
                       reads=[f"ps{b}"], writes=[f"ystage{i % 2}"])
        sch.dma("sp", call("dma_start", out=dst_ap, in_=st[0:nrows, :]), reads=[f"ystage{i % 2}"], writes=["y"])

    for i in range(S // 128):
        store_block(y_p[i * 128:(i + 1) * 128, :], 128, i * 128, i)
    store_block(y_s[:, :], NS, S, 16)

    sch.finish()
    sch.emit()
    return nc, sch


def make_consts():
    ident = np.eye(128, dtype=np.float32)
    invcnt = np.zeros((128, 4, 16), np.float32)
    for g in range(4):
        w = 2 << g
        for t in range(16):
            invcnt[:, g, t] = 1.0 / min(t + 1, w)
    return ident, invcnt


def t5_bucket_np(dist):
    dist = np.asarray(dist, np.int64)
    df = np.maximum(dist, 1).astype(np.float32)
    large = 16 + (np.log(df / np.float32(16)) / np.float32(np.log(2048 / 16)) * np.float32(16)).astype(np.int32)
    large = np.minimum(large, 31)
    return np.where(dist < 16, dist, large)


def make_ohf():
    ohf = np.zeros((32, 3, 384), np.float32)
    for g, dil in enumerate((1, 4, 16)):
        steps = np.arange(129)
        bk = t5_bucket_np(dil * steps)
        for st_, b in zip(steps, bk):
            ohf[b, g, 128 + st_] = 1.0
    return ohf


def make_sample_tables():
    ohm = np.zeros((32, 3, 4, 128), np.float32)
    ohx = np.zeros((32, 3, 4, NS), np.float32)
    for t in range(4):
        for i in range(128):
            j = 124 + t - i
            if 0 <= j <= 128:
                ohm[t5_bucket_np(j), 0, t, i] = 1.0
            ohm[t5_bucket_np(4 * (127 - i)), 1, t, i] = 1.0
            ohm[t5_bucket_np(16 * (127 - i)), 2, t, i] = 1.0
    for bt in range(NS):
        t = bt % 4
        for r in range(4):
            if r >= t:
                ohx[t5_bucket_np(128 + t - r), 0, r, bt] = 1.0
            if r == t:
                ohx[t5_bucket_np(4 * 128), 1, r, bt] = 1.0
                ohx[t5_bucket_np(16 * 128), 2, r, bt] = 1.0
    selc = np.zeros((128, NS, NS), np.float32)
    for bt in range(NS):
        selc[:, bt, bt] = 1.0
    selr = np.zeros((NS, NS, 128), np.float32)
    for bt in range(NS):
        selr[bt, bt, :] = 1.0
    return ohm, ohx, selc, selr


def make_in_maps(inp, n_cores=NCORES):
    ident, invcnt = make_consts()
    ohf = make_ohf()
    ohm, ohx, selc, selr = make_sample_tables()
    vec = np.zeros((NVEC, D), np.float32)
    vec[V_GAIN:V_GAIN + 16] = inp["norm_gains"].reshape(16, D)
    vec[V_PSCALE:V_PSCALE + 2] = inp["pool_scale"]
    vec[V_B1:V_B1 + 2] = inp["conv_b_pw1"].reshape(2, D)
    vec[V_BDW] = inp["conv_b_dw"][0]
    vec[V_LNG] = inp["conv_ln_g"][0]
    vec[V_LNB] = inp["conv_ln_b"][0]
    vec[V_B2] = inp["conv_b_pw2"][0]
    vec[V_WDW:V_WDW + 31] = inp["conv_w_dw"][0]
    maps = []
    for c in range(n_cores):
        m = {
            "x_p": np.ascontiguousarray(inp["x_prompt"][c]),
            "x_s": np.ascontiguousarray(inp["x_sample"][4 * c:4 * c + 4].reshape(NS, D)),
            "st_pool": np.ascontiguousarray(inp["state_pool"][:, 4 * c:4 * c + 4]),
            "vec_rows": vec,
            "ident_f": ident,
            "invcnt": invcnt,
            "pool_w": np.ascontiguousarray(inp["pool_w"]),
            "st_conv": np.ascontiguousarray(inp["state_conv"][0, 4 * c:4 * c + 4]),
            "w_pw1": np.ascontiguousarray(inp["conv_w_pw1"][0]),
            "w_pw2": np.ascontiguousarray(inp["conv_w_pw2"][0]),
            "w_qkv": np.ascontiguousarray(inp["attn_w_qkv"][0]),
            "w_o": np.ascontiguousarray(inp["attn_w_o"][0]),
            "rel_bias": np.ascontiguousarray(inp["rel_bias"]),
            "ohf": ohf,
            "ohm": ohm,
            "ohx": ohx,
            "selc": selc,
            "selr": selr,
            "cache_kv0": np.ascontiguousarray(inp["cache_kv_g0"][0, 4 * c:4 * c + 4].reshape(4, 128, 2048)),
            "cache_kv1": np.ascontiguousarray(inp["cache_kv_g1"][0, 4 * c:4 * c + 4].reshape(4, 512, 2048)),
            "cache_kv2": np.ascontiguousarray(inp["cache_kv_g2"][0, 4 * c:4 * c + 4].reshape(4, 2048, 2048)),
            "w_up": np.ascontiguousarray(inp["ffn_w_up"]),
            "w_down": np.ascontiguousarray(inp["ffn_w_down"]),
        }
        maps.append(m)
    return maps


_PROG = {}


def run(inp, n_cores=NCORES, layers=(0, 1, 2, 3), trace=False):
    key = tuple(layers)
    if key not in _PROG:
        _PROG[key] = build_program(layers)
    nc, _ = _PROG[key]
    maps = make_in_maps(inp, n_cores)
    res = run_bass_kernel_spmd(nc, maps, core_ids=list(range(n_cores)), trace=trace)
    return res


def kernel(**inp):
    inp = {k: np.asarray(v) for k, v in inp.items()}
    res = run(inp)
    R = res.results
    n = NCORES
    y_p = np.stack([R[c]["y_p"] for c in range(n)])
    y_s = np.concatenate([R[c]["y_s"].reshape(4, 4, D) for c in range(n)])
    pool_p = np.stack([R[c]["o_pool_p"] for c in range(n)], axis=1)
    pool_s = np.concatenate([R[c]["o_pool_s"] for c in range(n)], axis=1)
    outs = [y_p, y_s, pool_p, pool_s]
    for g, win in enumerate((128, 512, 2048)):
        kp = np.stack([R[c][f"o_kv_p{g}"].reshape(win, 2, 16, 64) for c in range(n)])[None]
        ks = np.concatenate([R[c][f"o_kv_s{g}"].reshape(4, win, 2, 16, 64) for c in range(n)])[None]
        outs += [kp, ks]
    conv_p = np.stack([R[c]["o_conv_p"] for c in range(n)])[None]
    conv_s = np.concatenate([R[c]["o_conv_s"] for c in range(n)])[None]
    outs += [conv_p, conv_s]
    return tuple(np.ascontiguousarray(o, dtype=np.float32) for o in outs)
```

```python
import numpy as np
import concourse.bass as bass
import concourse.mybir as mybir
from concourse.bass_utils import run_bass_kernel_spmd

F32 = mybir.dt.float32
BF16 = mybir.dt.bfloat16
ALU = mybir.AluOpType
AF = mybir.ActivationFunctionType

D = 1024
NCH = 8
S = 2048
NS = 16
T = S + NS
DFF = 4096
EPS = 1e-6
PADL = 32
NCORES = 8
ENGS = ("pe", "act", "dve", "pool", "sp")
NDMASEM = 64
DEBUG = {}
SB_BASE = 16640
SB_SIZE = 229376 - SB_BASE


def call(name, *a, **k):
    return (name, a, k)


class Sched:
    EPOCH = 12000

    def __init__(self, nc):
        self.nc = nc
        self.prog = {e: [] for e in ENGS}
        self.cnt = {e: 0 for e in ENGS}
        self.esems = {e: [] for e in ENGS}
        self.seen = {e: {} for e in ENGS}
        self.lastw = {}
        self.readers = {}
        self.dsems = [nc.alloc_semaphore(f"dq{i}") for i in range(NDMASEM)]
        self.dcnt = [0] * NDMASEM
        self.dpool = {"sp": list(range(0, 24)), "pool": list(range(24, 40)), "act": list(range(40, NDMASEM))}
        self.drr = {"sp": 0, "pool": 0, "act": 0}
        self.pe_pending = []
        self.nops = 0

    def _esem(self, e, ep):
        while len(self.esems[e]) <= ep:
            self.esems[e].append(self.nc.alloc_semaphore(f"e_{e}_{len(self.esems[e])}"))
        return self.esems[e][ep]

    def _handle(self, key):
        if key[0] == "c":
            return self._esem(key[1], key[2])
        return self.dsems[key[1]]

    def _need_wait(self, eng, tok):
        key, val = tok
        if key[0] == "c":
            k2 = ("c", key[1])
            cur = self.seen[eng].get(k2, (-1, 0))
            return (key[2], val) > cur
        cur = self.seen[eng].get(key, 0)
        return val > cur

    def _mark(self, eng, tok):
        key, val = tok
        if key[0] == "c":
            self.seen[eng][("c", key[1])] = (key[2], val)
        else:
            self.seen[eng][key] = val

    def _wait(self, eng, tok):
        if tok is None or not self._need_wait(eng, tok):
            return
        self.prog[eng].append(("w", self._handle(tok[0]), tok[1]))
        self._mark(eng, tok)

    def _deps(self, eng, reads, writes, is_dma):
        deps = []
        for r in reads:
            t = self.lastw.get(r)
            if t is not None:
                deps.append((t, "raw"))
        for w in writes:
            t = self.lastw.get(w)
            if t is not None:
                deps.append((t, "waw"))
            for t in self.readers.get(w, ()):
                deps.append((t, "war"))
        out = []
        for t, kind in deps:
            key = t[0]
            if (not is_dma) and key[0] == "c" and key[1] == eng:
                if eng == "pe" or kind != "raw":
                    continue
            out.append(t)
        return out

    def _record(self, tok, reads, writes):
        for r in reads:
            self.readers.setdefault(r, []).append(tok)
        for w in writes:
            self.lastw[w] = tok
            self.readers[w] = []

    def op(self, eng, fn, reads=(), writes=(), signal=True):
        self.nops += 1
        reads = tuple(reads)
        writes = tuple(writes)
        for t in self._deps(eng, reads, writes, False):
            self._wait(eng, t)
        if not signal:
            assert eng == "pe"
            self.prog[eng].append(("o", fn, None, 0))
            self.pe_pending.append((reads, writes))
            return None
        self.cnt[eng] += 1
        ep, val = divmod(self.cnt[eng] - 1, self.EPOCH)
        val += 1
        tok = (("c", eng, ep), val)
        self.prog[eng].append(("o", fn, self._esem(eng, ep), 1))
        if eng == "pe":
            for (r, w) in self.pe_pending:
                self._record(tok, r, w)
            self.pe_pending = []
        self._record(tok, reads, writes)
        return tok

    def dma(self, q, fn, reads=(), writes=()):
        self.nops += 1
        reads = tuple(reads)
        writes = tuple(writes)
        pl = self.dpool[q]
        i = pl[self.drr[q] % len(pl)]
        self.drr[q] += 1
        if self.dcnt[i] > 0:
            self._wait(q, (("d", i), 16 * self.dcnt[i]))
        for t in self._deps(q, reads, writes, True):
            self._wait(q, t)
        self.dcnt[i] += 1
        tok = (("d", i), 16 * self.dcnt[i])
        self.prog[q].append(("o", fn, self.dsems[i], 16))
        self._record(tok, reads, writes)
        return tok

    def barrier(self):
        assert not self.pe_pending
        toks = []
        for e in ENGS:
            if e != "sp" and self.cnt[e] > 0:
                ep, val = divmod(self.cnt[e] - 1, self.EPOCH)
                toks.append((("c", e, ep), val + 1))
        for i in range(NDMASEM):
            if self.dcnt[i] > 0:
                toks.append((("d", i), 16 * self.dcnt[i]))
        for e in ENGS:
            for t in toks:
                self._wait(e, t)

    def finish(self):
        for i in range(NDMASEM):
            if self.dcnt[i] > 0:
                self._wait("sp", (("d", i), 16 * self.dcnt[i]))
        for e in ENGS:
            if e != "sp" and self.cnt[e] > 0:
                ep, val = divmod(self.cnt[e] - 1, self.EPOCH)
                self._wait("sp", (("c", e, ep), val + 1))

    def emit(self):
        nc = self.nc
        with nc.Block() as block:
            for e, deco in (("pe", block.tensor), ("act", block.scalar), ("dve", block.vector),
                            ("pool", block.gpsimd), ("sp", block.sync)):
                items = self.prog[e]

                def body(eng, items=items):
                    for it in items:
                        if it[0] == "w":
                            eng.wait_ge(it[1], it[2])
                        else:
                            nm, a, k = it[1]
                            ins = getattr(eng, nm)(*a, **k)
                            if it[2] is not None:
                                ins.then_inc(it[2], it[3])
                deco(body)


class SB:
    def __init__(self, nc):
        self.nc = nc
        self.n = 0

    def at(self, off, shape, dtype, name=None):
        self.n += 1
        return self.nc.alloc_sbuf_tensor_at(name or f"sb{self.n}", list(shape), dtype, offset=off + SB_BASE)


def nbytes(shape, dtype):
    n = 1
    for s in shape[1:]:
        n *= s
    return n * (4 if dtype == F32 else 2)


class Region:
    def __init__(self, sb, start, end, sch=None):
        self.sb, self.start, self.end, self.cur = sb, start, end, start
        self.sch = sch

    def alloc(self, shape, dtype, name=None):
        off = (self.cur + 63) // 64 * 64
        nb = nbytes(shape, dtype)
        assert off + nb <= self.end, (name, off, nb, self.end)
        self.cur = off + nb
        return self.sb.at(off, shape, dtype, name)

    def reset(self):
        self.cur = self.start
        if self.sch is not None:
            self.sch.barrier()

    def mark(self):
        return self.cur

    def release(self, mark):
        self.cur = mark
        if self.sch is not None:
            self.sch.barrier()


V_GAIN = 0
V_PSCALE = 16
V_B1 = 18
V_BDW = 20
V_LNG = 21
V_LNB = 22
V_B2 = 23
V_WDW = 24
NVEC = 64


def build_program(layers=(0, 1, 2, 3)):
    nc = bass.Bass("TRN2", target_bir_lowering=False)
    sch = Sched(nc)

    def din(name, shape, dt=F32):
        return nc.dram_tensor(name, list(shape), dt, kind="ExternalInput")

    def dout(name, shape, dt=F32):
        return nc.dram_tensor(name, list(shape), dt, kind="ExternalOutput")

    x_p = din("x_p", [S, D])
    x_s = din("x_s", [NS, D])
    st_pool = din("st_pool", [2, 4, 15, D])
    vec_rows = din("vec_rows", [NVEC, D])
    ident_f_d = din("ident_f", [128, 128])
    invcnt_d = din("invcnt", [128, 4, 16])
    pool_w = din("pool_w", [2, 4, 256, 256])
    w_up = din("w_up", [4, D, DFF])
    w_down = din("w_down", [4, DFF, D])

    y_p = dout("y_p", [S, D])
    y_s = dout("y_s", [NS, D])
    o_pool_p = dout("o_pool_p", [2, 15, D])
    o_pool_s = dout("o_pool_s", [2, 4, 15, D])
    st_conv = din("st_conv", [4, 30, D])
    w_pw1 = din("w_pw1", [D, 2 * D])
    w_pw2 = din("w_pw2", [D, D])
    o_conv_p = dout("o_conv_p", [30, D])
    o_conv_s = dout("o_conv_s", [4, 30, D])
    WINS = (128, 512, 2048)
    DILS = (1, 4, 16)
    RS = 2 * D
    w_qkv = din("w_qkv", [D, 9 * D])
    w_o = din("w_o", [D, D])
    rel_bias_d = din("rel_bias", [32, 48])
    ohf_d = din("ohf", [32, 3, 384])
    ohm_d = din("ohm", [32, 3, 4, 128])
    ohx_d = din("ohx", [32, 3, 4, NS])
    selc_d = din("selc", [128, NS, NS])
    selr_d = din("selr", [NS, NS, 128])
    cache_kv = [din(f"cache_kv{g}", [4, WINS[g], RS]) for g in range(3)]
    o_kv_p = [dout(f"o_kv_p{g}", [WINS[g], RS]) for g in range(3)]
    o_kv_s = [dout(f"o_kv_s{g}", [4, WINS[g], RS]) for g in range(3)]
    xt_scr = nc.dram_tensor("xt_scr", [128, NCH * T], F32)
    escr = nc.dram_tensor("escr", [3, 16, 128, 384], BF16)
    qkvs_scr = nc.dram_tensor("qkvs_scr", [NS, 9 * D], F32)

    sb = SB(nc)
    off = 0

    def persist(shape, dtype, name):
        nonlocal off
        off = (off + 63) // 64 * 64
        t = sb.at(off, shape, dtype, name)
        off += nbytes(shape, dtype)
        return t

    XT = persist([128, NCH, T], F32, "XT")
    ident_f = persist([128, 128], F32, "ident_f")
    ident_b = persist([128, 128], BF16, "ident_b")
    ones_m = persist([128, 128], BF16, "ones_m")
    vecs = persist([128, NCH, NVEC], F32, "vecs")
    invcnt = persist([128, 4, 16], F32, "invcnt")
    epsc = persist([128, 16], F32, "epsc")
    rstd = [persist([128, 528], F32, f"rstd{i}") for i in range(2)]
    WR_SLOTS = 4
    WR_BYTES = 8192
    wring_off = (off + 63) // 64 * 64
    off = wring_off + WR_SLOTS * WR_BYTES
    PH0 = (off + 63) // 64 * 64
    PH_END = SB_SIZE
    reg = Region(sb, PH0, PH_END, sch)

    PS = [nc.alloc_psum_tensor(f"ps{i}", [128, 512], F32) for i in range(8)]
    ps_rr = [0]
    ps_reserved = set()

    def next_bank():
        while True:
            b = ps_rr[0]
            ps_rr[0] = (b + 1) % 8
            if b not in ps_reserved:
                return b

    wr_rr = [0]

    def wslot(shape, dtype=BF16):
        i = wr_rr[0]
        wr_rr[0] = (i + 1) % WR_SLOTS
        assert nbytes(shape, dtype) <= WR_BYTES
        return i, sb.at(wring_off + i * WR_BYTES, shape, dtype)

    def gvec(row, c):
        return vecs[:, c, row:row + 1]

    dumps = {}

    def dump(name, t, shape, res):
        if not DEBUG.get("dump"):
            return
        d = nc.dram_tensor("dbg_" + name, list(shape), F32, kind="ExternalOutput")
        dumps[name] = d
        nd = len(shape)
        sl = tuple(slice(None) for _ in range(nd))
        sch.dma("sp", call("dma_start", out=d[sl], in_=t[sl]), reads=[res], writes=["dbg_" + name])

    sch.dma("sp", call("dma_start", out=ident_f[:, :], in_=ident_f_d[:, :]), writes=["ident_f"])
    sch.dma("sp", call("dma_start", out=invcnt[:, :, :], in_=invcnt_d[:, :, :]), writes=["invcnt"])
    sch.op("dve", call("tensor_copy", out=ident_b[:, :], in_=ident_f[:, :]), reads=["ident_f"], writes=["ident_b"])
    sch.op("pool", call("memset", ones_m[:, :], 1.0 / D), writes=["ones_m"])
    sch.op("pool", call("memset", epsc[:, :], EPS), writes=["epsc"])

    reg.reset()
    vstage = reg.alloc([NVEC, D], F32, "vstage")
    sch.dma("sp", call("dma_start", out=vstage[:, :], in_=vec_rows[:, :]), writes=["vstage"])
    for c in range(NCH):
        b = next_bank()
        sch.op("pe", call("transpose", out=PS[b][:, 0:NVEC], in_=vstage[:, c * 128:(c + 1) * 128],
                                                     identity=ident_f[0:NVEC, 0:NVEC]),
               reads=["vstage", "ident_f"], writes=[f"ps{b}"])
        sch.op("dve", call("tensor_copy", out=vecs[:, c, :], in_=PS[b][:, 0:NVEC]),
               reads=[f"ps{b}"], writes=["vecs"])

    xstage = [reg.alloc([128, D], F32, f"xstage{i}") for i in range(2)]

    def load_block(src_ap, nrows, col0, i):
        st = xstage[i % 2]
        sch.dma("sp", call("dma_start", out=st[0:nrows, :], in_=src_ap), writes=[f"xstage{i % 2}"])
        for c0 in range(0, NCH, 4):
            b = next_bank()
            for c in range(c0, c0 + 4):
                sch.op("pe", call("transpose",
                    out=PS[b][:, (c - c0) * 128:(c - c0) * 128 + nrows], in_=st[0:nrows, c * 128:(c + 1) * 128],
                    identity=ident_f[0:nrows, 0:nrows]),
                    reads=[f"xstage{i % 2}", "ident_f"], writes=[f"ps{b}"], signal=(c == c0 + 3))
            eng = "dve" if (c0 == 0) else "act"
            if eng == "dve":
                sch.op("dve", call("tensor_copy",
                    out=XT[:, c0:c0 + 4, col0:col0 + nrows],
                    in_=PS[b][:, :].rearrange("p (c t) -> p c t", c=4)[:, :, 0:nrows]),
                    reads=[f"ps{b}"], writes=["XT"])
            else:
                sch.op("act", call("copy",
                    out=XT[:, c0:c0 + 4, col0:col0 + nrows],
                    in_=PS[b][:, :].rearrange("p (c t) -> p c t", c=4)[:, :, 0:nrows]),
                    reads=[f"ps{b}"], writes=["XT"])

    for i in range(S // 128):
        load_block(x_p[i * 128:(i + 1) * 128, :], 128, i * 128, i)
    load_block(x_s[:, :], NS, S, 16)

    TILES = [[(0, 512)], [(512, 512)], [(1024, 512)], [(1536, 512), (2048, NS)]]
    if DEBUG.get("no_segB"):
        TILES[3] = [(1536, 512)]

    def sumsq_rstd(src_fn, segs, sq, rs, res_reads, eps=EPS):
        o = 0
        for si, (c0, n) in enumerate(segs):
            for c in range(NCH):
                sch.op("act", call("activation", out=sq[:, c, o:o + n], in_=src_fn(c, si), func=AF.Square),
                       reads=(res_reads(c) if callable(res_reads) else res_reads), writes=[sq.name + str(c)])
            b = next_bank()
            for c in range(NCH):
                sch.op("pe", call("matmul", PS[b][:, 0:n], lhsT=ones_m[:, :], rhs=sq[:, c, o:o + n],
                                                                  start=(c == 0), stop=(c == NCH - 1)),
                       reads=[sq.name + str(c), "ones_m"], writes=[f"ps{b}"], signal=(c == NCH - 1))
            if DEBUG.get("arsqrt", False):
                sch.op("act", call("activation", out=rs[:, o:o + n], in_=PS[b][:, 0:n], func=AF.Abs_reciprocal_sqrt,
                                   bias=epsc[:, 0:1], scale=1.0), reads=[f"ps{b}", "epsc"], writes=[rs.name])
            else:
                sch.op("act", call("activation", out=rs[:, o:o + n], in_=PS[b][:, 0:n], func=AF.Sqrt,
                                   bias=epsc[:, 0:1], scale=1.0), reads=[f"ps{b}", "epsc"], writes=[rs.name])
                sch.op("dve", call("reciprocal", out=rs[:, o:o + n], in_=rs[:, o:o + n]),
                       reads=[rs.name], writes=[rs.name])
            o += n

    def prenorm(gain_row, segs, sq, rs, out_fn, out_res):
        sumsq_rstd(lambda c, si: XT[:, c, segs[si][0]:segs[si][0] + segs[si][1]], segs, sq, rs, ["XT"])
        o = 0
        for si, (c0, n) in enumerate(segs):
            for c in range(NCH):
                sch.op("dve", call("scalar_tensor_tensor",
                    out=out_fn(c, si), in0=XT[:, c, c0:c0 + n], scalar=gvec(V_GAIN + gain_row, c),
                    in1=rs[:, o:o + n], op0=ALU.mult, op1=ALU.mult),
                    reads=["XT", rs.name, "vecs"], writes=[out_res(c) if callable(out_res) else out_res])
            o += n

    def postnorm_residual(gain_row, segs, mst, sq, rs, mres=None):
        if mres is None:
            mres = lambda c: "mst"
        offs = []
        o = 0
        for (c0, n) in segs:
            offs.append(o)
            o += n
        sumsq_rstd(lambda c, si: mst[:, c, offs[si]:offs[si] + segs[si][1]], segs, sq, rs, lambda c: [mres(c)])
        for si, (c0, n) in enumerate(segs):
            o = offs[si]
            for c in range(NCH):
                eng = "dve"
                sch.op(eng, call("tensor_tensor", out=mst[:, c, o:o + n], in0=mst[:, c, o:o + n],
                                                                    in1=rs[:, o:o + n], op=ALU.mult),
                       reads=[mres(c), rs.name], writes=[mres(c)])
                sch.op("dve", call("scalar_tensor_tensor",
                    out=XT[:, c, c0:c0 + n], in0=mst[:, c, o:o + n], scalar=gvec(V_GAIN + gain_row, c),
                    in1=XT[:, c, c0:c0 + n], op0=ALU.mult, op1=ALU.add),
                    reads=[mres(c), "vecs", "XT"], writes=["XT"])

    def load_w(dram_ap, shape):
        i, t = wslot(shape)
        sch.dma("pool", call("dma_start", out=t[:, :, :], in_=dram_ap), writes=[f"wr{i}"])
        return i, t

    class WStream:
        def __init__(self, pieces):
            self.pieces = pieces
            self.issued = 0
            self.slots = {}

        def get(self, i, pf=WR_SLOTS - 1):
            upto = min(i + 1 + pf, len(self.pieces))
            while self.issued < upto:
                ap, shape = self.pieces[self.issued]
                self.slots[self.issued] = load_w(ap, shape)
                self.issued += 1
            return self.slots.pop(i)

    def ffn(layer):
        reg.reset()
        aT = reg.alloc([128, 32, 528], BF16, "aT")
        h2 = [reg.alloc([128, NCH, 528], BF16, f"h2T{i}") for i in range(2)]
        mst = reg.alloc([128, NCH, 528], F32, "mst")
        sqA = reg.alloc([128, NCH, 528], BF16, "sqA")
        sqB = reg.alloc([128, NCH, 528], BF16, "sqB")
        rl = [reg.alloc([128, 528], BF16, f"rl{i}") for i in range(3)]
        wu = w_up[layer].rearrange("(k p) f -> p k f", p=128)
        wd = w_down[layer].rearrange("(f p) d -> p f d", p=128)
        tiles = [ti for ti in range(4) if DEBUG.get("ffn_tiles") is None or ti in DEBUG["ffn_tiles"]]
        pieces = []
        for ti in tiles:
            for fp in range(8):
                pieces.append((wu[:, :, fp * 512:(fp + 1) * 512], [128, NCH, 512]))
            for dh in range(2):
                for fp in range(4):
                    pieces.append((wd[:, fp * 8:(fp + 1) * 8, dh * 512:(dh + 1) * 512], [128, 8, 512]))
        ws = WStream(pieces)
        pc = [0]
        offs = [0, 512]

        def do_prenorm(ti):
            segs = TILES[ti]
            h2T = h2[ti % 2]
            views = [[h2T[:, c, offs[si]:offs[si] + n] for si, (c0, n) in enumerate(segs)] for c in range(NCH)]
            prenorm(layer * 4 + 2, segs, sqA, rstd[ti % 2], lambda c, si, v=views: v[c][si], h2T.name)

        do_prenorm(tiles[0])
        rln = [0]
        for idx, ti in enumerate(tiles):
            segs = TILES[ti]
            h2T = h2[ti % 2]
            hasB = len(segs) > 1
            bankB = next_bank() if hasB else None
            if hasB:
                ps_reserved.add(bankB)
            for fp in range(8):
                wi, wt = ws.get(pc[0])
                pc[0] += 1
                for fl in range(4):
                    fc = fp * 4 + fl
                    b = next_bank()
                    for k in range(NCH):
                        sch.op("pe", call("matmul", PS[b][:, 0:512], lhsT=wt[:, k, fl * 128:(fl + 1) * 128], rhs=h2T[:, k, 0:512],
                                          start=(k == 0), stop=(k == NCH - 1)),
                               reads=[f"wr{wi}", h2T.name], writes=[f"ps{b}"], signal=(k == NCH - 1))
                    if hasB:
                        for k in range(NCH):
                            sch.op("pe", call("matmul", PS[bankB][:, fc * NS:(fc + 1) * NS], lhsT=wt[:, k, fl * 128:(fl + 1) * 128],
                                              rhs=h2T[:, k, 512:512 + NS], start=(k == 0), stop=(k == NCH - 1)),
                                   reads=[f"wr{wi}", h2T.name], writes=[f"ps{bankB}"], signal=(k == NCH - 1))
                    r = rl[rln[0] % 3]
                    rln[0] += 1
                    sch.op("act", call("activation", out=r[:, 0:512], in_=PS[b][:, 0:512], func=AF.Relu),
                           reads=[f"ps{b}"], writes=[r.name])
                    sch.op("dve", call("tensor_tensor", out=aT[:, fc, 0:512], in0=r[:, 0:512], in1=r[:, 0:512], op=ALU.mult),
                           reads=[r.name], writes=["aT"])
            if hasB:
                r = rl[rln[0] % 3]
                rln[0] += 1
                sch.op("act", call("activation", out=r[:, 0:512], in_=PS[bankB][:, 0:512], func=AF.Relu),
                       reads=[f"ps{bankB}"], writes=[r.name])
                sch.op("dve", call("tensor_tensor", out=aT[:, :, 512:512 + NS],
                                   in0=r[:, 0:512].rearrange("p (f t) -> p f t", t=NS),
                                   in1=r[:, 0:512].rearrange("p (f t) -> p f t", t=NS), op=ALU.mult),
                       reads=[r.name], writes=["aT"])
                ps_reserved.discard(bankB)
            if idx + 1 < len(tiles):
                do_prenorm(tiles[idx + 1])
            for dh in range(2):
                banks = [next_bank() for _ in range(4)]
                bBs = [next_bank() for _ in range(4)] if hasB else None
                for fp in range(4):
                    wi, wt = ws.get(pc[0])
                    pc[0] += 1
                    for dl in range(4):
                        b = banks[dl]
                        for fl in range(8):
                            f = fp * 8 + fl
                            sch.op("pe", call("matmul", PS[b][:, 0:512], lhsT=wt[:, fl, dl * 128:(dl + 1) * 128], rhs=aT[:, f, 0:512],
                                              start=(f == 0), stop=(f == 31)),
                                   reads=[f"wr{wi}", "aT"], writes=[f"ps{b}"], signal=(fl == 7))
                        if hasB:
                            for fl in range(8):
                                f = fp * 8 + fl
                                sch.op("pe", call("matmul", PS[bBs[dl]][:, 0:NS], lhsT=wt[:, fl, dl * 128:(dl + 1) * 128],
                                                  rhs=aT[:, f, 512:512 + NS], start=(f == 0), stop=(f == 31)),
                                       reads=[f"wr{wi}", "aT"], writes=[f"ps{bBs[dl]}"], signal=(fl == 7))
                for dl in range(4):
                    b = banks[dl]
                    c = dh * 4 + dl
                    if dl % 2 == 0:
                        sch.op("act", call("copy", out=mst[:, c, 0:512], in_=PS[b][:, 0:512]), reads=[f"ps{b}"], writes=[f"mst{c}"])
                    else:
                        sch.op("dve", call("tensor_copy", out=mst[:, c, 0:512], in_=PS[b][:, 0:512]), reads=[f"ps{b}"], writes=[f"mst{c}"])
                if hasB:
                    for dl in range(4):
                        sch.op("dve", call("tensor_copy", out=mst[:, dh * 4 + dl, 512:512 + NS], in_=PS[bBs[dl]][:, 0:NS]),
                               reads=[f"ps{bBs[dl]}"], writes=[f"mst{dh * 4 + dl}"])
            postnorm_residual(layer * 4 + 3, segs, mst, sqB, rstd[ti % 2], mres=lambda c: f"mst{c}")

    def pool_layer(layer, j):
        reg.reset()
        hT = reg.alloc([128, NCH, PADL + T], BF16, "hT")
        pooled = hT[:, :, PADL:PADL + T]
        tA = reg.alloc([128, 16 + S], F32, "tA")
        tB = reg.alloc([128, 16 + S], F32, "tB")
        tC = reg.alloc([128, 16 + S], F32, "tC")
        tD = reg.alloc([128, 16 + S], F32, "tD")
        corr2 = reg.alloc([128, 16], F32, "corr2")
        hse = reg.alloc([128, NCH, 4, 19], F32, "hse")
        corr = reg.alloc([128, 16], F32, "corr")
        sq = reg.alloc([128, NCH, 528], BF16, "sq")
        mst = None
        sch.op("pool", call("memset", hT[:, :, 0:PADL], 0.0), writes=[f"hTc{c}" for c in range(NCH)])
        for ti, segs in enumerate(TILES):
            rs = rstd[ti % 2]
            prenorm(layer * 4 + 0, segs, sq, rs,
                    lambda c, si: hT[:, c, PADL + segs[si][0]:PADL + segs[si][0] + segs[si][1]], lambda c: f"hTc{c}")
            if ti == 3:
                for c in range(NCH):
                    sch.op("dve", call("scalar_tensor_tensor",
                        out=hse[:, c, :, 15:19], in0=XT[:, c, S:S + NS].rearrange("p (b t) -> p b t", t=4),
                        scalar=gvec(V_GAIN + layer * 4, c), in1=rs[:, 512:512 + NS].rearrange("p (b t) -> p b t", t=4),
                        op0=ALU.mult, op1=ALU.mult), reads=["XT", rs.name, "vecs"], writes=["hse"])
                hp15 = reg.alloc([128, NCH, 16], F32, "hp15")
                for c in range(NCH):
                    sch.op("dve", call("scalar_tensor_tensor",
                        out=hp15[:, c, :], in0=XT[:, c, S - 16:S], scalar=gvec(V_GAIN + layer * 4, c),
                        in1=rs[:, 512 - 16:512], op0=ALU.mult, op1=ALU.mult),
                        reads=["XT", rs.name, "vecs"], writes=["hp15"])
        dump(f"rs{layer}", rstd[1], [128, 528], "rstd1")
        dump(f"hp15_{layer}", hp15, [128, NCH, 16], "hp15")
        dump(f"vecs{layer}", vecs, [128, NCH, NVEC], "vecs")
        sst = reg.alloc([128, D], F32, "sst")
        sch.dma("sp", call("dma_start", out=sst[0:60, :], in_=st_pool[j].rearrange("b r d -> (b r) d")),
                writes=["sst"])
        for c in range(NCH):
            b = next_bank()
            sch.op("pe", call("transpose", out=PS[b][:, 0:60], in_=sst[0:60, c * 128:(c + 1) * 128],
                                                         identity=ident_f[0:60, 0:60]),
                   reads=["sst", "ident_f"], writes=[f"ps{b}"])
            sch.op("act", call("copy", out=hse[:, c, :, 0:15],
                                                    in_=PS[b][:, 0:60].rearrange("p (b r) -> p b r", r=15)),
                   reads=[f"ps{b}"], writes=["hse"])
        ost = sst
        b = next_bank()
        b2 = next_bank()
        for c in range(NCH):
            bb = b if c < 4 else b2
            sch.op("pe", call("transpose", out=PS[bb][0:15, (c % 4) * 128:(c % 4 + 1) * 128],
                                                           in_=hp15[:, c, 1:16], identity=ident_f[:, :]),
                   reads=["hp15", "ident_f"], writes=[f"ps{bb}"], signal=(c % 4 == 3))
        sch.op("dve", call("tensor_copy", out=ost[0:15, 0:512], in_=PS[b][0:15, :]), reads=[f"ps{b}"], writes=["sst"])
        sch.op("dve", call("tensor_copy", out=ost[0:15, 512:1024], in_=PS[b2][0:15, :]), reads=[f"ps{b2}"], writes=["sst"])
        sch.dma("sp", call("dma_start", out=o_pool_p[j, :, :], in_=ost[0:15, :]), reads=["sst"], writes=["o_pool_p"])
        ost2 = sst
        hsc = reg.alloc([128, NCH, 60], F32, "hsc")
        for c in range(NCH):
            sch.op("dve", call("tensor_copy", out=hsc[:, c, :].rearrange("p (b r) -> p b r", r=15),
                                                       in_=hse[:, c, :, 4:19]), reads=["hse"], writes=["hsc"])
        b = next_bank()
        b2 = next_bank()
        for c in range(NCH):
            bb = b if c < 4 else b2
            sch.op("pe", call("transpose", out=PS[bb][0:60, (c % 4) * 128:(c % 4 + 1) * 128],
                                                           in_=hsc[:, c, :], identity=ident_f[:, :]),
                   reads=["hsc", "ident_f"], writes=[f"ps{bb}"], signal=(c % 4 == 3))
        sch.op("dve", call("tensor_copy", out=ost2[0:60, 0:512], in_=PS[b][0:60, :]), reads=[f"ps{b}"], writes=["sst"])
        sch.op("dve", call("tensor_copy", out=ost2[0:60, 512:1024], in_=PS[b2][0:60, :]), reads=[f"ps{b2}"], writes=["sst"])
        sch.dma("sp", call("dma_start", out=o_pool_s[j].rearrange("b r d -> (b r) d"), in_=ost2[0:60, :]),
                reads=["sst"], writes=["o_pool_s"])
        dump(f"hse{layer}", hse, [128, NCH, 4, 19], "hse")
        dump(f"hsc{layer}", hsc, [128, NCH, 60], "hsc")
        dump(f"ost2_{layer}", ost2, [64, D], "sst")
        dump(f"sst{layer}", sst, [64, D], "sst")
        for c in range(NCH):
            g = c // 2
            w = 2 << g
            eng = "pool" if c < 3 else "dve"
            hp = hT[:, c, PADL - 16:PADL + S]
            cur = None
            bufs = [tA, tB] if eng == "dve" else [tC, tD]
            res = ["tA", "tB"] if eng == "dve" else ["tC", "tD"]
            corr_ = corr if eng == "dve" else corr2
            corr_n = "corr" if eng == "dve" else "corr2"
            sh = 1
            step = 0
            while sh < w:
                dst = bufs[step % 2]
                if cur is None:
                    sch.op(eng, call("tensor_tensor",
                        out=dst[:, 16:16 + S], in0=hp[:, 16:16 + S], in1=hp[:, 16 - sh:16 - sh + S], op=ALU.add),
                        reads=[f"hTc{c}"], writes=[res[step % 2]])
                    sch.op(eng, call("memset", dst[:, 0:16], 0.0), writes=[res[step % 2]])
                else:
                    src = cur
                    sch.op(eng, call("tensor_tensor",
                        out=dst[:, 16:16 + S], in0=src[:, 16:16 + S], in1=src[:, 16 - sh:16 - sh + S], op=ALU.add),
                        reads=[res[(step + 1) % 2]], writes=[res[step % 2]])
                    sch.op(eng, call("memset", dst[:, 0:16], 0.0), writes=[res[step % 2]])
                cur = dst
                cres = res[step % 2]
                sh *= 2
                step += 1
            sch.op(eng, call("tensor_tensor", out=corr_[:, :], in0=cur[:, 16:32], in1=invcnt[:, g, :], op=ALU.mult),
                   reads=[cres, "invcnt"], writes=[corr_n])
            sch.op(eng, call("tensor_tensor", out=pooled[:, c, 0:16], in0=corr_[:, :], in1=hp[:, 16:32], op=ALU.subtract),
                   reads=[corr_n, f"hTc{c}"], writes=[f"hTc{c}"])
            sch.op("dve", call("scalar_tensor_tensor", out=pooled[:, c, 16:S], in0=cur[:, 32:16 + S], scalar=1.0 / w,
                               in1=hp[:, 32:16 + S], op0=ALU.mult, op1=ALU.subtract),
                   reads=[cres, f"hTc{c}"], writes=[f"hTc{c}"])
        if DEBUG.get("dump"):
            pdump = reg.alloc([128, NCH, 32], F32, "pdump")
            sch.op("dve", call("tensor_copy", out=pdump[:, :, :], in_=hT[:, :, PADL:PADL + 32]), reads=[f"hTc{c}" for c in range(NCH)], writes=["pdump"])
            dump(f"pooled{layer}", pdump, [128, NCH, 32], "pdump")
        hs2 = reg.alloc([128, NCH, 4, 19], F32, "hs2")
        for c in range(NCH):
            w = 2 << (c // 2)
            sch.op("dve", call("tensor_tensor", out=hs2[:, c, :, 15:19], in0=hse[:, c, :, 15:19],
                                                        in1=hse[:, c, :, 14:18], op=ALU.add),
                   reads=["hse"], writes=["hs2"])
            for k in range(2, w):
                sch.op("dve", call("tensor_tensor", out=hs2[:, c, :, 15:19], in0=hs2[:, c, :, 15:19],
                                                                 in1=hse[:, c, :, 15 - k:19 - k], op=ALU.add),
                       reads=["hse", "hs2"], writes=["hs2"])
            sch.op("dve", call("scalar_tensor_tensor",
                out=pooled[:, c, S:S + NS].rearrange("p (b t) -> p b t", t=4), in0=hs2[:, c, :, 15:19],
                scalar=1.0 / w, in1=hse[:, c, :, 15:19], op0=ALU.mult, op1=ALU.subtract),
                reads=["hs2", "hse"], writes=[f"hTc{c}"])
        wi, wt = load_w(pool_w[j].rearrange("g (k p) d -> p (g k) d", p=128), [128, 8, 256])
        mst = reg.alloc([128, NCH, 528], F32, "mst")
        for ti, segs in enumerate(TILES):
            rs = rstd[ti % 2]
            o = 0
            for si, (c0, n) in enumerate(segs):
                for dc in range(NCH):
                    g = dc // 2
                    b = next_bank()
                    for k in range(2):
                        sch.op("pe", call("matmul",
                            PS[b][:, 0:n], lhsT=wt[:, 2 * g + k, (dc % 2) * 128:(dc % 2 + 1) * 128],
                            rhs=pooled[:, 2 * g + k, c0:c0 + n], start=(k == 0), stop=(k == 1)),
                            reads=[f"wr{wi}", f"hTc{2 * g + k}"], writes=[f"ps{b}"], signal=(k == 1))
                    sch.op("act", call("activation",
                        out=mst[:, dc, o:o + n], in_=PS[b][:, 0:n], func=AF.Copy, scale=gvec(V_PSCALE + j, dc)),
                        reads=[f"ps{b}", "vecs"], writes=["mst"])
                o += n
            postnorm_residual(layer * 4 + 1, segs, mst, sq, rs)

    def conv_layer(layer):
        reg.reset()
        PADU = 32
        uT = reg.alloc([128, NCH, PADU + S], BF16, "uT")
        use = reg.alloc([128, NCH, 4, 34], BF16, "use")
        ufp = reg.alloc([128, NCH, 48], F32, "ufp")
        sq = reg.alloc([128, NCH, 528], BF16, "sq")
        mk = reg.mark()
        hT = reg.alloc([128, NCH, T], BF16, "hT")
        sg = [reg.alloc([128, 512], F32, f"sg{i}") for i in range(2)]
        sst = reg.alloc([128, D], F32, "sst")
        sch.op("pool", call("memset", uT[:, :, 0:PADU], 0.0), writes=["uT"])
        for ti, segs in enumerate(TILES):
            rs = rstd[ti % 2]
            hviews = [[hT[:, c, c0:c0 + n] for (c0, n) in segs] for c in range(NCH)]
            prenorm(layer * 4 + 0, segs, sq, rs, lambda c, si, hv=hviews: hv[c][si], "hT")
        sch.dma("sp", call("dma_start", out=sst[0:120, :], in_=st_conv.rearrange("b r d -> (b r) d")), writes=["sst"])
        for c in range(NCH):
            b = next_bank()
            sch.op("pe", call("transpose", out=PS[b][:, 0:120], in_=sst[0:120, c * 128:(c + 1) * 128],
                              identity=ident_f[0:120, 0:120]), reads=["sst", "ident_f"], writes=[f"ps{b}"])
            sch.op("act", call("copy", out=use[:, c, :, 0:30], in_=PS[b][:, 0:120].rearrange("p (b r) -> p b r", r=30)),
                   reads=[f"ps{b}"], writes=["use"])
        sch.dma("sp", call("dma_start", out=o_conv_s[:, 0:26, :], in_=st_conv[:, 4:30, :]), writes=["o_conv_s"])
        w1v = w_pw1.rearrange("(k p) f -> p k f", p=128)
        wts = [load_w(w1v[:, :, q * 512:(q + 1) * 512], [128, NCH, 512]) for q in range(4)]

        def glu_tile(c0, n, u_out_fn, ufp_fn):
            for c in range(NCH):
                ba = next_bank()
                bg = next_bank()
                for which, bb, cc in ((0, ba, c), (1, bg, c + 8)):
                    wi, wt = wts[cc // 4]
                    for k in range(NCH):
                        sch.op("pe", call("matmul", PS[bb][:, 0:n], lhsT=wt[:, k, (cc % 4) * 128:(cc % 4 + 1) * 128],
                                          rhs=hT[:, k, c0:c0 + n], start=(k == 0), stop=(k == NCH - 1)),
                               reads=[f"wr{wi}", "hT"], writes=[f"ps{bb}"], signal=(k == NCH - 1))
                g_ = sg[c % 2]
                sch.op("act", call("activation", out=g_[:, 0:n], in_=PS[bg][:, 0:n], func=AF.Sigmoid,
                                   bias=gvec(V_B1 + 1, c), scale=1.0), reads=[f"ps{bg}", "vecs"], writes=[g_.name])
                sch.op("dve", call("scalar_tensor_tensor", out=u_out_fn(c), in0=PS[ba][:, 0:n], scalar=gvec(V_B1, c),
                                   in1=g_[:, 0:n], op0=ALU.add, op1=ALU.mult),
                       reads=[f"ps{ba}", g_.name, "vecs"], writes=["uT"])
                if ufp_fn is not None:
                    lo, cnt, dst = ufp_fn(c)
                    sch.op("dve", call("scalar_tensor_tensor", out=dst, in0=PS[ba][:, lo:lo + cnt], scalar=gvec(V_B1, c),
                                       in1=g_[:, lo:lo + cnt], op0=ALU.add, op1=ALU.mult),
                           reads=[f"ps{ba}", g_.name, "vecs"], writes=["ufp"])

        for ti in range(4):
            c0 = ti * 512
            glu_tile(c0, 512, lambda c, c0=c0: uT[:, c, PADU + c0:PADU + c0 + 512],
                     (lambda c: (480, 32, ufp[:, c, 0:32])) if ti == 3 else None)
        glu_tile(S, NS, lambda c: use[:, c, :, 30:34], lambda c: (0, NS, ufp[:, c, 32:48]))
        ost = reg.alloc([128, D], F32, "ost")
        for (lo, cnt, dst_aps) in ((2, 30, [(o_conv_p[:, :], 0, 30)]),
                                   (32, 16, [(o_conv_s[bq, 26:30, :], 4 * bq, 4) for bq in range(4)])):
            b = next_bank()
            b2 = next_bank()
            for c in range(NCH):
                bb = b if c < 4 else b2
                sch.op("pe", call("transpose", out=PS[bb][0:cnt, (c % 4) * 128:(c % 4 + 1) * 128],
                                  in_=ufp[:, c, lo:lo + cnt], identity=ident_f[:, :]),
                       reads=["ufp", "ident_f"], writes=[f"ps{bb}"], signal=(c % 4 == 3))
            sch.op("dve", call("tensor_copy", out=ost[0:cnt, 0:512], in_=PS[b][0:cnt, :]), reads=[f"ps{b}"], writes=["ost"])
            sch.op("dve", call("tensor_copy", out=ost[0:cnt, 512:1024], in_=PS[b2][0:cnt, :]), reads=[f"ps{b2}"], writes=["ost"])
            for (dap, r0, nr) in dst_aps:
                sch.dma("sp", call("dma_start", out=dap, in_=ost[r0:r0 + nr, :]), reads=["ost"], writes=["o_conv"])
        reg.release(mk)
        Dc = [reg.alloc([128, 31, 128], BF16, f"Dc{i}") for i in range(2)]
        mst = reg.alloc([128, NCH, 528], F32, "mst")
        ybf = reg.alloc([128, NCH, 528], BF16, "ybf")
        zin = reg.alloc([128, NCH, 528], BF16, "zin")
        mu = reg.alloc([128, 528], F32, "mu")
        tmp = [reg.alloc([128, 528], F32, f"ctmp{i}") for i in range(3)]
        w2v = w_pw2.rearrange("(k p) f -> p k f", p=128)
        w2 = [load_w(w2v[:, :, q * 512:(q + 1) * 512], [128, NCH, 512]) for q in range(2)]
        dcn = [0]

        def build_diag(seq):
            c = seq % NCH
            dt_ = Dc[seq % 2]
            for jt in range(31):
                if jt % 2 == 0:
                    sch.op("act", call("activation", out=dt_[:, jt, :], in_=ident_b[:, :], func=AF.Copy,
                                       scale=vecs[:, c, V_WDW + jt:V_WDW + jt + 1]),
                           reads=["ident_b", "vecs"], writes=[dt_.name + "a"])
                else:
                    sch.op("dve", call("tensor_scalar", out=dt_[:, jt, :], in0=ident_b[:, :],
                                       scalar1=vecs[:, c, V_WDW + jt:V_WDW + jt + 1], scalar2=None, op0=ALU.mult),
                           reads=["ident_b", "vecs"], writes=[dt_.name + "d"])

        NSEQ = len(TILES) * NCH
        build_diag(0)
        for ti, segs in enumerate(TILES):
            rs = rstd[ti % 2]
            offs = [0, 512]
            for c in range(NCH):
                seq = dcn[0]
                dt_ = Dc[seq % 2]
                dcn[0] += 1
                if seq + 1 < NSEQ:
                    build_diag(seq + 1)
                for si, (c0, n) in enumerate(segs):
                    b = next_bank()
                    for jt in range(31):
                        if si == 0:
                            rhs = uT[:, c, PADU - 30 + c0 + jt:PADU - 30 + c0 + jt + n]
                            outp = PS[b][:, 0:n]
                        else:
                            rhs = use[:, c, :, jt:jt + 4]
                            outp = PS[b][:, 0:NS].rearrange("p (b t) -> p b t", t=4)
                        sch.op("pe", call("matmul", outp, lhsT=dt_[:, jt, :], rhs=rhs, start=(jt == 0), stop=(jt == 30)),
                               reads=[dt_.name + "a", dt_.name + "d", "uT", "use"], writes=[f"ps{b}"], signal=(jt == 30))
                    o = offs[si]
                    sch.op("act", call("activation", out=mst[:, c, o:o + n], in_=PS[b][:, 0:n], func=AF.Identity,
                                       bias=gvec(V_BDW, c), scale=1.0), reads=[f"ps{b}", "vecs"], writes=["mst"])
                    sch.op("act", call("activation", out=ybf[:, c, o:o + n], in_=PS[b][:, 0:n], func=AF.Identity,
                                       bias=gvec(V_BDW, c), scale=1.0), reads=[f"ps{b}", "vecs"], writes=["ybf"])
                    sch.op("dve", call("tensor_tensor", out=sq[:, c, o:o + n], in0=mst[:, c, o:o + n],
                                       in1=mst[:, c, o:o + n], op=ALU.mult), reads=["mst"], writes=[sq.name + str(c)])
            for si, (c0, n) in enumerate(segs):
                o = offs[si]
                bm = next_bank()
                bv = next_bank()
                for c in range(NCH):
                    sch.op("pe", call("matmul", PS[bm][:, 0:n], lhsT=ones_m[:, :], rhs=ybf[:, c, o:o + n],
                                      start=(c == 0), stop=(c == NCH - 1)),
                           reads=["ybf", "ones_m"], writes=[f"ps{bm}"], signal=(c == NCH - 1))
                for c in range(NCH):
                    sch.op("pe", call("matmul", PS[bv][:, 0:n], lhsT=ones_m[:, :], rhs=sq[:, c, o:o + n],
                                      start=(c == 0), stop=(c == NCH - 1)),
                           reads=[sq.name + str(c), "ones_m"], writes=[f"ps{bv}"], signal=(c == NCH - 1))
                sch.op("act", call("copy", out=mu[:, o:o + n], in_=PS[bm][:, 0:n]), reads=[f"ps{bm}"], writes=["mu"])
                t0_ = tmp[0]
                sch.op("dve", call("tensor_tensor", out=t0_[:, o:o + n], in0=mu[:, o:o + n], in1=mu[:, o:o + n], op=ALU.mult),
                       reads=["mu"], writes=[t0_.name])
                sch.op("dve", call("tensor_tensor", out=t0_[:, o:o + n], in0=PS[bv][:, 0:n], in1=t0_[:, o:o + n],
                                   op=ALU.subtract), reads=[f"ps{bv}", t0_.name], writes=[t0_.name])
                sch.op("act", call("activation", out=rs[:, o:o + n], in_=t0_[:, o:o + n], func=AF.Sqrt,
                                   bias=epsc[:, 0:1], scale=1.0), reads=[t0_.name, "epsc"], writes=[rs.name])
                sch.op("dve", call("reciprocal", out=rs[:, o:o + n], in_=rs[:, o:o + n]), reads=[rs.name], writes=[rs.name])
                for c in range(NCH):
                    t1 = tmp[1 + c % 2]
                    sch.op("dve", call("tensor_tensor", out=t1[:, o:o + n], in0=mst[:, c, o:o + n], in1=mu[:, o:o + n],
                                       op=ALU.subtract), reads=["mst", "mu"], writes=[t1.name])
                    sch.op("dve", call("tensor_tensor", out=t1[:, o:o + n], in0=t1[:, o:o + n], in1=rs[:, o:o + n],
                                       op=ALU.mult), reads=[t1.name, rs.name], writes=[t1.name])
                    sch.op("act", call("activation", out=zin[:, c, o:o + n], in_=t1[:, o:o + n], func=AF.Silu,
                                       bias=gvec(V_LNB, c), scale=gvec(V_LNG, c)),
                           reads=[t1.name, "vecs"], writes=["zin"])
            for si, (c0, n) in enumerate(segs):
                o = offs[si]
                for dc in range(NCH):
                    wi, wt = w2[dc // 4]
                    b = next_bank()
                    for k in range(NCH):
                        sch.op("pe", call("matmul", PS[b][:, 0:n], lhsT=wt[:, k, (dc % 4) * 128:(dc % 4 + 1) * 128],
                                          rhs=zin[:, k, o:o + n], start=(k == 0), stop=(k == NCH - 1)),
                               reads=[f"wr{wi}", "zin"], writes=[f"ps{b}"], signal=(k == NCH - 1))
                    sch.op("act", call("activation", out=mst[:, dc, o:o + n], in_=PS[b][:, 0:n], func=AF.Identity,
                                       bias=gvec(V_B2, dc), scale=1.0), reads=[f"ps{b}", "vecs"], writes=["mst"])
            postnorm_residual(layer * 4 + 1, segs, mst, sq, rs)

    def attn_layer(layer):
        reg.reset()
        hT = reg.alloc([128, NCH, T], BF16, "hT")
        oT = reg.alloc([128, NCH, T], BF16, "oT")
        sq = reg.alloc([128, NCH, 528], BF16, "sq")
        m0 = reg.mark()
        rb = reg.alloc([32, 48], F32, "rb")
        expb = reg.alloc([32, 48], F32, "expb")
        ohf = reg.alloc([32, 3, 384], F32, "ohf")
        fsb = reg.alloc([16, 3, 384], BF16, "fsb")
        sch.dma("sp", call("dma_start", out=rb[:, :], in_=rel_bias_d[:, :]), writes=["rb"])
        sch.dma("sp", call("dma_start", out=ohf[:, :, :], in_=ohf_d[:, :, :]), writes=["ohf"])
        rbb = reg.alloc([32, 48], BF16, "rbb")
        ohfb = reg.alloc([32, 3, 384], BF16, "ohfb")
        sch.op("act", call("activation", out=expb[:, :], in_=rb[:, :], func=AF.Exp), reads=["rb"], writes=["expb"])
        sch.op("dve", call("tensor_copy", out=rbb[:, :], in_=expb[:, :]), reads=["expb"], writes=["rbb"])
        sch.op("dve", call("tensor_copy", out=ohfb[:, :, :], in_=ohf[:, :, :]), reads=["ohf"], writes=["ohfb"])
        for g in range(3):
            b = next_bank()
            sch.op("pe", call("matmul", PS[b][0:16, 0:384], lhsT=rbb[:, g * 16:(g + 1) * 16], rhs=ohfb[:, g, :],
                              start=True, stop=True), reads=["rbb", "ohfb"], writes=[f"ps{b}"])
            sch.op("dve", call("tensor_copy", out=fsb[:, g, :], in_=PS[b][0:16, 0:384]), reads=[f"ps{b}"], writes=["fsb"])
            sch.dma("sp", call("dma_start", out=escr[g, :, :, :],
                               in_=fsb[:, g, :].unsqueeze(1).broadcast_to([16, 128, 384])),
                    reads=["fsb"], writes=["escr"])
        for ti, segs in enumerate(TILES):
            rs = rstd[ti % 2]
            hviews = [[hT[:, c, c0:c0 + n] for (c0, n) in segs] for c in range(NCH)]
            prenorm(layer * 4 + 0, segs, sq, rs, lambda c, si, hv=hviews: hv[c][si], "hT")
        sch.dma("sp", call("dma_start", out=xt_scr[:, :], in_=XT[:, :, :].rearrange("p c t -> p (c t)")),
                reads=["XT"], writes=["xt_scr"])
        sch.barrier()
        copy_chunks = []
        for g in range(3):
            win = WINS[g]
            rows = win - 4
            nchunk = (rows + 255) // 256
            for bq in range(4):
                for ch in range(nchunk):
                    r0 = rows * ch // nchunk
                    r1 = rows * (ch + 1) // nchunk
                    copy_chunks.append((g, bq, r0, r1))
        copy_chunks.sort(key=lambda x: -x[0])

        def issue_copies(n):
            for _ in range(n):
                if not copy_chunks:
                    return
                g, bq, r0, r1 = copy_chunks.pop(0)
                sch.dma("act", call("dma_start", out=o_kv_s[g][bq, r0:r1, :], in_=cache_kv[g][bq, 4 + r0:4 + r1, :]),
                        writes=[f"o_kv_s{g}"])
        rx = Region(sb, 0, nbytes([128, NCH, T], F32))
        reg.cur = m0
        mk = reg.mark()
        ones64 = reg.alloc([128, 128], BF16, "ones64")
        sch.op("pool", call("memset", ones64[:, :], 1.0), writes=["ones64"])
        QB = [rx.alloc([128, S], BF16, f"qb{i}") for i in range(2)]
        KM = [[rx.alloc([128, S], BF16, f"km{i}{h}") for h in range(2)] for i in range(2)]
        for i in range(2):
            sch.op("pool", call("memset", KM[i][0][64:128, :], 0.0), writes=[f"km{i}0"])
            sch.op("pool", call("memset", KM[i][1][0:64, :], 0.0), writes=[f"km{i}1"])
        qk_n = [0]
        OL = rx.alloc([128, 2, S], F32, "OL")
        Vb = [reg.alloc([128, 16, 128], BF16, f"vb{g}") for g in range(3)]
        Ep = [reg.alloc([128, 3, 2, 256], BF16, f"ep{i}") for i in range(2)]
        Pt = [reg.alloc([128, 512], BF16, f"pt{i}") for i in range(4)]
        kvst = [rx.alloc([128, 16, 128], F32, f"kvst{i}") for i in range(2)]
        qst = reg.alloc([NS, 9, 128], F32, "qst")
        NW = 16
        wq = [sb.at(wring_off + i * 2048, [128, NCH, 128], BF16) for i in range(NW)]
        wq_rr = [0]
        wv = w_qkv.rearrange("(k p) f -> p k f", p=128)
        kvst_n = [0]
        pt_n = [0]

        def load_wblk(col0):
            i = wq_rr[0]
            wq_rr[0] = (i + 1) % NW
            sch.dma("pool", call("dma_start", out=wq[i][:, :, :], in_=wv[:, :, col0:col0 + 128]), writes=[f"wq{i}"])
            return i, wq[i]

        NPAIR = DEBUG.get("attn_pairs", NCH)
        wnext = None
        for j in range(NPAIR):
            ep = Ep[j % 2]
            PARTS = DEBUG.get("attn_parts", "ewsft")
            for g in range(3):
                if "e" not in PARTS:
                    break
                src = bass.AP(escr, ((g * 16 + 2 * j) * 128 * 384) + 128, [[383, 128], [128 * 384, 2], [1, 256]])
                sch.dma("sp", call("dma_start", out=ep[:, g, :, :], in_=src), reads=["escr"], writes=[ep.name])
            order = [(g, w) for g in range(3) for w in range(3)]
            NPF = NW - 9
            if j == 0:
                wnext = {}
            wblk = wnext
            for (g, w) in order:
                if (g, w) not in wblk:
                    wblk[(g, w)] = load_wblk((3 * g + w) * D + j * 128)
            wnext = {}
            if j + 1 < NPAIR:
                for (g, w) in order[:NPF]:
                    wnext[(g, w)] = load_wblk((3 * g + w) * D + (j + 1) * 128)
            for g in range(3):
                if "s" not in PARTS:
                    break
                b = next_bank()
                for w in range(3):
                    wi, wt = wblk[(g, w)]
                    for k in range(NCH):
                        sch.op("pe", call("matmul", PS[b][0:NS, w * 128:(w + 1) * 128], lhsT=hT[:, k, S:S + NS], rhs=wt[:, k, :],
                                          start=(k == 0), stop=(k == NCH - 1)),
                               reads=[f"wq{wi}", "hT"], writes=[f"ps{b}"], signal=(k == NCH - 1))
                sch.op("act", call("copy", out=qst[:, 3 * g:3 * g + 3, :],
                                   in_=PS[b][0:NS, 0:384].rearrange("p (w c) -> p w c", c=128)),
                       reads=[f"ps{b}"], writes=["qst"])
            dst = bass.AP(qkvs_scr, j * 128, [[9 * D, NS], [D, 9], [1, 128]])
            if "s" in PARTS:
                sch.dma("sp", call("dma_start", out=dst, in_=qst[:, :, :]), reads=["qst"], writes=["qkvs_scr"])
            for g in range(3):
                dil = DILS[g]
                L = S // dil
                nblk = L // 128
                issue_copies(2)
                qi = qk_n[0] % 2
                qk_n[0] += 1
                for w in range(2):
                    if "f" not in PARTS:
                        break
                    wi, wt = wblk[(g, w)]
                    for tt in range(4):
                        b = next_bank()
                        for k in range(NCH):
                            sch.op("pe", call("matmul", PS[b][:, 0:512], lhsT=wt[:, k, :], rhs=hT[:, k, tt * 512:(tt + 1) * 512],
                                              start=(k == 0), stop=(k == NCH - 1)),
                                   reads=[f"wq{wi}", "hT"], writes=[f"ps{b}"], signal=(k == NCH - 1))
                        lw = 512 // dil
                        if w == 0:
                            dstv = QB[qi][:, :].rearrange("p (r l) -> p r l", r=dil)
                            sch.op("dve", call("tensor_copy", out=dstv[:, :, tt * lw:(tt + 1) * lw],
                                               in_=PS[b][:, 0:512].rearrange("p (l r) -> p r l", r=dil)),
                                   reads=[f"ps{b}"], writes=[f"qb{qi}"])
                        else:
                            for h in range(2):
                                rows = slice(h * 64, (h + 1) * 64)
                                dstv = KM[qi][h][rows, :].rearrange("p (r l) -> p r l", r=dil)
                                sch.op("act", call("copy", out=dstv[:, :, tt * lw:(tt + 1) * lw],
                                                   in_=PS[b][rows, 0:512].rearrange("p (l r) -> p r l", r=dil)),
                                       reads=[f"ps{b}"], writes=[f"km{qi}{h}"])
                for w in (2, 1):
                    if "t" not in PARTS:
                        break
                    wi, wt = wblk[(g, w)]
                    if w == 2:
                        tiles = list(range(16))
                    elif g == 0:
                        tiles = [15]
                    elif g == 1:
                        tiles = [3, 7, 11, 15]
                    else:
                        tiles = list(range(16))
                    st = kvst[kvst_n[0] % 2]
                    kvst_n[0] += 1
                    for q0 in range(0, len(tiles), 4):
                        grp = tiles[q0:q0 + 4]
                        b = next_bank()
                        for gi, n in enumerate(grp):
                            r, blk = divmod(n, nblk)
                            t0 = r + dil * 128 * blk
                            for k in range(NCH):
                                sch.op("pe", call("matmul", PS[b][:, gi * 128:(gi + 1) * 128],
                                                  lhsT=hT[:, k, t0:t0 + dil * 127 + 1:dil], rhs=wt[:, k, :],
                                                  start=(k == 0), stop=(k == NCH - 1)),
                                       reads=[f"wq{wi}", "hT"], writes=[f"ps{b}"],
                                       signal=(k == NCH - 1 and gi == len(grp) - 1))
                        ng = len(grp)
                        contiguous = all(grp[i] == grp[0] + i for i in range(ng))
                        if contiguous:
                            sch.op("act", call("copy", out=st[:, grp[0]:grp[0] + ng, :],
                                               in_=PS[b][:, 0:ng * 128].rearrange("p (n c) -> p n c", c=128)),
                                   reads=[f"ps{b}"], writes=[st.name])
                            if w == 2:
                                sch.op("dve", call("tensor_copy", out=Vb[g][:, grp[0]:grp[0] + ng, :],
                                                   in_=st[:, grp[0]:grp[0] + ng, :]),
                                       reads=[st.name], writes=[f"vb{g}"])
                        else:
                            for gi, n in enumerate(grp):
                                sch.op("act", call("copy", out=st[:, n, :], in_=PS[b][:, gi * 128:(gi + 1) * 128]),
                                       reads=[f"ps{b}"], writes=[st.name])
                    col = (w - 1) * D + j * 128
                    if g == 0:
                        dsts = [(bass.AP(o_kv_p[0], col, [[RS, 128], [1, 128]]), st[:, 15, :])]
                    elif g == 1:
                        dsts = [(bass.AP(o_kv_p[1], col, [[4 * RS, 128], [RS, 4], [1, 128]]), st[:, 3:16:4, :])]
                    else:
                        dsts = [(bass.AP(o_kv_p[2], col, [[16 * RS, 128], [RS, 16], [1, 128]]), st[:, :, :])]
                    for (dap, sap) in dsts:
                        sch.dma("sp", call("dma_start", out=dap, in_=sap), reads=[st.name], writes=[f"o_kv_p{g}"])
                blocks = [(r, qb) for r in range(dil) for qb in range(nblk)]

                def colof(r, blk):
                    return (r * nblk + blk) * 128

                def emit_qk(r, qb):
                    bS = next_bank()
                    p_ = Pt[pt_n[0] % 4]
                    pt_n[0] += 1
                    nparts = 2 if qb > 0 else 1
                    cq = colof(r, qb)
                    for h in range(2):
                        rows = slice(h * 64, (h + 1) * 64)
                        for part in range(nparts):
                            ck = colof(r, qb - part)
                            last = (h == 1 and part == nparts - 1)
                            sch.op("pe", call("matmul", PS[bS][:, h * 256 + part * 128:h * 256 + part * 128 + 128],
                                              lhsT=KM[qi][h][:, ck:ck + 128], rhs=QB[qi][:, cq:cq + 128],
                                              start=True, stop=True),
                                   reads=[f"qb{qi}", f"km{qi}{h}"], writes=[f"ps{bS}"], signal=last)
                    if nparts == 2:
                        sch.op("act", call("activation", out=p_[:, :], in_=PS[bS][:, :], func=AF.Exp, scale=0.125),
                               reads=[f"ps{bS}"], writes=[p_.name])
                        sch.op("dve", call("tensor_tensor", out=p_[:, :], in0=p_[:, :],
                                           in1=ep[:, g, :, :].rearrange("p h q -> p (h q)"), op=ALU.mult),
                               reads=[p_.name, ep.name], writes=[p_.name])
                    else:
                        pv = p_[:, :].rearrange("p (h q) -> p h q", h=2)[:, :, 0:128]
                        sch.op("act", call("activation", out=pv, in_=PS[bS][:, :].rearrange("p (h q) -> p h q", h=2)[:, :, 0:128],
                                           func=AF.Exp, scale=0.125), reads=[f"ps{bS}"], writes=[p_.name])
                        sch.op("dve", call("tensor_tensor", out=pv, in0=pv, in1=ep[:, g, :, 0:128], op=ALU.mult),
                               reads=[p_.name, ep.name], writes=[p_.name])
                    return p_, nparts

                def emit_pv(r, qb, p_, nparts):
                    if DEBUG.get("attn_nopv"):
                        return
                    bO = next_bank()
                    pview = p_[:, :].rearrange("p (h x q) -> p h x q", h=2, x=2)
                    for which in range(2):
                        for pi, part in enumerate(range(nparts - 1, -1, -1)):
                            n = r * nblk + qb - part
                            lhsT = Vb[g][:, n, :] if which == 0 else ones64[:, :]
                            last = (which == 1 and pi == nparts - 1)
                            sch.op("pe", call("matmul", PS[bO][:, which * 256:(which + 1) * 256].rearrange("p (h q) -> p h q", h=2),
                                              lhsT=lhsT, rhs=pview[:, :, part, :], start=(pi == 0), stop=(pi == nparts - 1)),
                                   reads=[p_.name, f"vb{g}", "ones64"], writes=[f"ps{bO}"], signal=last)
                    t0 = r + dil * 128 * qb
                    for h in range(2):
                        rows = slice(h * 64, (h + 1) * 64)
                        ov = OL[rows, :, t0:t0 + dil * 127 + 1:dil]
                        pin = PS[bO][rows, :].rearrange("p (w h q) -> p w h q", w=2, h=2)[:, :, h, :]
                        if g == 0:
                            sch.op("dve", call("tensor_copy", out=ov, in_=pin), reads=[f"ps{bO}"], writes=["OL"])
                        else:
                            sch.op("dve", call("tensor_tensor", out=ov, in0=ov, in1=pin, op=ALU.add),
                                   reads=[f"ps{bO}", "OL"], writes=["OL"])

                prev = None
                if DEBUG.get("attn_nocore"):
                    blocks = []
                    prev = "skip"
                pend = []
                for (r, qb) in blocks:
                    pend.append((r, qb) + emit_qk(r, qb))
                    if len(pend) > 2:
                        emit_pv(*pend.pop(0))
                while pend:
                    emit_pv(*pend.pop(0))
            sch.op("act", call("activation", out=OL[:, 1, :], in_=OL[:, 1, :], func=AF.Ln), reads=["OL"], writes=["OL"])
            sch.op("act", call("activation", out=OL[:, 1, :], in_=OL[:, 1, :], func=AF.Exp, scale=-1.0),
                   reads=["OL"], writes=["OL"])
            sch.op("dve", call("tensor_tensor", out=oT[:, j, 0:S], in0=OL[:, 0, :], in1=OL[:, 1, :], op=ALU.mult),
                   reads=["OL"], writes=["oT"])
        issue_copies(len(copy_chunks))
        sch.barrier()
        rx.cur = 0
        reg.cur = mk
        for g in range(3):
            win = WINS[g]
            for bq in range(4):
                sch.dma("sp", call("dma_start", out=o_kv_s[g][bq, win - 4:win, :],
                                   in_=qkvs_scr[4 * bq:4 * bq + 4, g * 3 * D + D:g * 3 * D + 3 * D]),
                        reads=["qkvs_scr"], writes=[f"o_kv_s{g}"])
        rb2 = reg.alloc([32, 48], F32, "rb2")
        rbb2 = reg.alloc([32, 48], BF16, "rbb2")
        ohm_f = rx.alloc([32, 3, 4, 128], F32, "ohm_f")
        ohm = reg.alloc([32, 3, 4, 128], BF16, "ohm")
        ohx_f = rx.alloc([32, 3, 4, NS], F32, "ohx_f")
        ohx = reg.alloc([32, 3, 4, NS], BF16, "ohx")
        selc_f = rx.alloc([128, NS, NS], F32, "selc_f")
        selc = reg.alloc([128, NS, NS], BF16, "selc")
        Em = reg.alloc([128, 3, 4, 16], F32, "Em")
        Ex = reg.alloc([NS, 3, 4, 16], F32, "Ex")
        sch.dma("sp", call("dma_start", out=rb2[:, :], in_=rel_bias_d[:, :]), writes=["rb2"])
        sch.dma("sp", call("dma_start", out=ohm_f[:, :, :, :], in_=ohm_d[:, :, :, :]), writes=["ohm_f"])
        sch.dma("sp", call("dma_start", out=ohx_f[:, :, :, :], in_=ohx_d[:, :, :, :]), writes=["ohx_f"])
        sch.dma("sp", call("dma_start", out=selc_f[:, :, :], in_=selc_d[:, :, :]), writes=["selc_f"])
        sch.op("act", call("activation", out=rb2[:, :], in_=rb2[:, :], func=AF.Exp), reads=["rb2"], writes=["rb2"])
        sch.op("dve", call("tensor_copy", out=rbb2[:, :], in_=rb2[:, :]), reads=["rb2"], writes=["rbb2"])
        sch.op("dve", call("tensor_copy", out=ohm[:, :, :, :].rearrange("p a b c -> p (a b c)"),
                           in_=ohm_f[:, :, :, :].rearrange("p a b c -> p (a b c)")), reads=["ohm_f"], writes=["ohm"])
        sch.op("dve", call("tensor_copy", out=ohx[:, :, :, :].rearrange("p a b c -> p (a b c)"),
                           in_=ohx_f[:, :, :, :].rearrange("p a b c -> p (a b c)")), reads=["ohx_f"], writes=["ohx"])
        sch.op("dve", call("tensor_copy", out=selc[:, :, :].rearrange("p a b -> p (a b)"),
                           in_=selc_f[:, :, :].rearrange("p a b -> p (a b)")), reads=["selc_f"], writes=["selc"])
        for g in range(3):
            b = next_bank()
            for t in range(4):
                sch.op("pe", call("matmul", PS[b][:, t * 16:(t + 1) * 16], lhsT=ohm[:, g, t, :], rhs=rbb2[:, g * 16:(g + 1) * 16],
                                  start=True, stop=True), reads=["ohm", "rbb2"], writes=[f"ps{b}"], signal=(t == 3))
            sch.op("dve", call("tensor_copy", out=Em[:, g, :, :].rearrange("p t h -> p (t h)"), in_=PS[b][:, 0:64]),
                   reads=[f"ps{b}"], writes=["Em"])
            b = next_bank()
            for r in range(4):
                sch.op("pe", call("matmul", PS[b][0:NS, r * 16:(r + 1) * 16], lhsT=ohx[:, g, r, :], rhs=rbb2[:, g * 16:(g + 1) * 16],
                                  start=True, stop=True), reads=["ohx", "rbb2"], writes=[f"ps{b}"], signal=(r == 3))
            sch.op("dve", call("tensor_copy", out=Ex[:, g, :, :].rearrange("p t h -> p (t h)"), in_=PS[b][0:NS, 0:64]),
                   reads=[f"ps{b}"], writes=["Ex"])
        bO1 = next_bank()
        bO2 = next_bank()
        bL = next_bank()
        ps_reserved.update([bO1, bO2, bL])
        NRING = 5
        KVt = [rx.alloc([128, RS], BF16, f"kvt{i}") for i in range(NRING)]
        qbt = [rx.alloc([128, D], BF16, f"qbt{i}") for i in range(NRING)]
        prod = [reg.alloc([128, D], F32, f"prod{i}") for i in range(2)]
        Zb = [reg.alloc([128, D], BF16, f"zb{i}") for i in range(2)]
        Ssc = [reg.alloc([128, 16], F32, f"Ssc{i}") for i in range(2)]
        Pm = [reg.alloc([128, 16], F32, f"Pm{i}") for i in range(2)]
        Pmb = [reg.alloc([128, 16], BF16, f"Pmb{i}") for i in range(2)]
        first = [True]
        AX = mybir.AxisListType.X

        def accum(lhsT, z, pb, res_reads, is_last):
            st_ = first[0]
            first[0] = False
            sch.op("pe", call("matmul", PS[bO1][0:NS, 0:512], lhsT=lhsT, rhs=z[:, 0:512], start=st_, stop=is_last),
                   reads=res_reads, writes=[f"ps{bO1}"], signal=False)
            sch.op("pe", call("matmul", PS[bO2][0:NS, 0:512], lhsT=lhsT, rhs=z[:, 512:1024], start=st_, stop=is_last),
                   reads=res_reads, writes=[f"ps{bO2}"], signal=False)
            sch.op("pe", call("matmul", PS[bL][0:NS, 0:16], lhsT=lhsT, rhs=pb, start=st_, stop=is_last),
                   reads=res_reads, writes=[f"ps{bL}"], signal=True)

        qall = reg.alloc([NS, 3, D], BF16, "qall")
        for g in range(3):
            sch.dma("pool", call("dma_start", out=qall[:, g, :], in_=qkvs_scr[:, g * 3 * D:g * 3 * D + D]),
                    reads=["qkvs_scr"], writes=["qall"])
        selr = reg.alloc([NS, NS, 128], BF16, "selr")
        sch.op("dve", call("tensor_copy", out=selr[:, :, :],
                           in_=ident_b[0:NS, 0:NS].unsqueeze(2).broadcast_to([NS, NS, 128])),
               reads=["ident_b"], writes=["selr"])
        items = [(g, bq, t) for g in range(3) for bq in range(4) for t in range(4)]
        loaded = [0]

        def issue_loads(upto):
            while loaded[0] < min(upto, len(items)):
                g, bq, t = items[loaded[0]]
                kv = KVt[loaded[0] % NRING]
                qq = qbt[loaded[0] % NRING]
                loaded[0] += 1
                dil = DILS[g]
                win = WINS[g]
                base = bq * win * RS
                if g == 0:
                    src = bass.AP(o_kv_s[0], base, [[RS, 128], [1, RS]])
                else:
                    src = bass.AP(o_kv_s[g], base + (win - 4 - dil * 127 + t) * RS, [[dil * RS, 128], [1, RS]])
                sch.dma("pool", call("dma_start", out=kv[:, :], in_=src), reads=[f"o_kv_s{g}"], writes=[kv.name])
                for hh in range(2):
                    bq_ = next_bank()
                    sch.op("pe", call("matmul", PS[bq_][:, 0:512], lhsT=selr[:, 4 * bq + t, :], rhs=qall[:, g, hh * 512:(hh + 1) * 512],
                                      start=True, stop=True), reads=["selr", "qall"], writes=[f"ps{bq_}"])
                    sch.op("act", call("copy", out=qq[:, hh * 512:(hh + 1) * 512], in_=PS[bq_][:, 0:512]),
                           reads=[f"ps{bq_}"], writes=[qq.name])

        for idx, (g, bq, t) in enumerate(items):
            issue_loads(idx + NRING)
            kv = KVt[idx % NRING]
            qq = qbt[idx % NRING]
            pr = prod[idx % 2]
            ss = Ssc[idx % 2]
            pm = Pm[idx % 2]
            pmb = Pmb[idx % 2]
            z = Zb[idx % 2]
            sch.op("dve", call("tensor_tensor", out=pr[:, :], in0=kv[:, 0:D], in1=qq[:, :], op=ALU.mult),
                   reads=[kv.name, qq.name], writes=[pr.name])
            sch.op("dve", call("tensor_reduce", out=ss[:, :], in_=pr[:, :].rearrange("p (h c) -> p h c", c=64),
                               axis=AX, op=ALU.add), reads=[pr.name], writes=[ss.name])
            sch.op("act", call("activation", out=pm[:, :], in_=ss[:, :], func=AF.Exp, scale=0.125),
                   reads=[ss.name], writes=[pm.name])
            sch.op("dve", call("tensor_tensor", out=pm[:, :], in0=pm[:, :], in1=Em[:, g, t, :], op=ALU.mult),
                   reads=[pm.name, "Em"], writes=[pm.name])
            sch.op("act", call("copy", out=pmb[:, :], in_=pm[:, :]), reads=[pm.name], writes=[pmb.name])
            sch.op("dve", call("tensor_tensor", out=z[:, :].rearrange("p (h c) -> p h c", c=64),
                               in0=kv[:, D:2 * D].rearrange("p (h c) -> p h c", c=64),
                               in1=pm[:, :].unsqueeze(2).broadcast_to([128, 16, 64]), op=ALU.mult),
                   reads=[kv.name, pm.name], writes=[z.name])
            accum(selc[:, 4 * bq + t, :], z, pmb[:, :], [z.name, pmb.name, "selc"], False)
        Kxt = rx.alloc([NS, 4, RS], BF16, "Kxt")
        Kx = Kxt[:, :, :]
        KXN = "Kxt"
        qxt = rx.alloc([NS, D], BF16, "qxt")
        qx = qxt[:, :]
        QXN = "qxt"
        Sx = reg.alloc([NS, 4, 16], F32, "Sx")
        Px = reg.alloc([NS, 4, 16], F32, "Px")
        Pxb = reg.alloc([NS, 4, 16], BF16, "Pxb")
        for g in range(3):
            win = WINS[g]
            for bq in range(4):
                src = bass.AP(cache_kv[g], bq * win * RS, [[0, 4], [RS, 4], [1, RS]])
                sch.dma("pool", call("dma_start", out=Kxt[4 * bq:4 * bq + 4, :, :], in_=src), writes=[KXN])
            sch.dma("pool", call("dma_start", out=qx, in_=qkvs_scr[:, g * 3 * D:g * 3 * D + D]),
                    reads=["qkvs_scr"], writes=[QXN])
            for r in range(4):
                pr = prod[r % 2]
                sch.op("dve", call("tensor_tensor", out=pr[0:NS, :], in0=Kx[:, r, 0:D], in1=qx, op=ALU.mult),
                       reads=[KXN, QXN], writes=[pr.name])
                sch.op("dve", call("tensor_reduce", out=Sx[:, r, :], in_=pr[0:NS, :].rearrange("p (h c) -> p h c", c=64),
                                   axis=AX, op=ALU.add), reads=[pr.name], writes=["Sx"])
            sch.op("act", call("activation", out=Px[:, :, :].rearrange("p t h -> p (t h)"),
                               in_=Sx[:, :, :].rearrange("p t h -> p (t h)"), func=AF.Exp, scale=0.125),
                   reads=["Sx"], writes=["Px"])
            sch.op("dve", call("tensor_tensor", out=Px[:, :, :], in0=Px[:, :, :], in1=Ex[:, g, :, :], op=ALU.mult),
                   reads=["Px", "Ex"], writes=["Px"])
            sch.op("dve", call("tensor_copy", out=Pxb[:, :, :], in_=Px[:, :, :]), reads=["Px"], writes=["Pxb"])
            for r in range(4):
                z = Zb[r % 2]
                sch.op("dve", call("tensor_tensor", out=z[0:NS, :].rearrange("p (h c) -> p h c", c=64),
                                   in0=Kx[:, r, D:2 * D].rearrange("p (h c) -> p h c", c=64),
                                   in1=Px[:, r, :].unsqueeze(2).broadcast_to([NS, 16, 64]), op=ALU.mult),
                       reads=[KXN, "Px"], writes=[z.name])
                accum(ident_b[0:NS, 0:NS], z[0:NS, :], Pxb[:, r, :], [z.name, "Pxb", "ident_b"], (g == 2 and r == 3))
        os_ = rx.alloc([NS, D], F32, "os_")
        lr = reg.alloc([NS, 16], F32, "lr")
        sch.op("dve", call("reciprocal", out=lr[:, :], in_=PS[bL][0:NS, 0:16]), reads=[f"ps{bL}"], writes=["lr"])
        for hh, bb in enumerate((bO1, bO2)):
            sch.op("dve", call("tensor_tensor", out=os_[:, hh * 512:(hh + 1) * 512].rearrange("p (h c) -> p h c", c=64),
                               in0=PS[bb][0:NS, 0:512].rearrange("p (h c) -> p h c", c=64),
                               in1=lr[:, hh * 8:(hh + 1) * 8].unsqueeze(2).broadcast_to([NS, 8, 64]), op=ALU.mult),
                   reads=[f"ps{bb}", "lr"], writes=["os_"])
        ps_reserved.clear()
        b = next_bank()
        for c in range(NCH):
            sch.op("pe", call("transpose", out=PS[b][:, c * NS:(c + 1) * NS], in_=os_[:, c * 128:(c + 1) * 128],
                              identity=ident_f[0:NS, 0:NS]), reads=["os_", "ident_f"], writes=[f"ps{b}"], signal=(c == NCH - 1))
        sch.op("dve", call("tensor_copy", out=oT[:, :, S:S + NS], in_=PS[b][:, 0:NCH * NS].rearrange("p (c t) -> p c t", t=NS)),
               reads=[f"ps{b}"], writes=["oT"])
        sch.barrier()
        sch.dma("sp", call("dma_start", out=XT[:, :, :].rearrange("p c t -> p (c t)"), in_=xt_scr[:, :]),
                reads=["xt_scr"], writes=["XT"])
        reg.release(mk)
        mst = reg.alloc([128, NCH, 528], F32, "mst")
        mst2 = sb.at(PH0, [128, NCH, 528], F32, "mstalt")
        wov = w_o.rearrange("(k p) f -> p k f", p=128)
        wo = [load_w(wov[:, :, q * 512:(q + 1) * 512], [128, NCH, 512]) for q in range(2)]
        for ti, segs in enumerate(TILES):
            rs = rstd[ti % 2]
            mt = mst if ti % 2 == 0 else mst2
            mname = (lambda c, ti=ti: f"mo{ti % 2}_{c}")
            o = 0
            for si, (c0, n) in enumerate(segs):
                for dc in range(NCH):
                    wi, wt = wo[dc // 4]
                    b = next_bank()
                    for k in range(NCH):
                        sch.op("pe", call("matmul", PS[b][:, 0:n], lhsT=wt[:, k, (dc % 4) * 128:(dc % 4 + 1) * 128],
                                          rhs=oT[:, k, c0:c0 + n], start=(k == 0), stop=(k == NCH - 1)),
                               reads=[f"wr{wi}", "oT"], writes=[f"ps{b}"], signal=(k == NCH - 1))
                    sch.op("act", call("copy", out=mt[:, dc, o:o + n], in_=PS[b][:, 0:n]), reads=[f"ps{b}"], writes=[mname(dc)])
                o += n
            postnorm_residual(layer * 4 + 1, segs, mt, sq, rs, mres=mname)

    pool_j = 0
    for layer in range(4):
        if layer not in layers:
            if layer % 3 == 0:
                pool_j += 1
            continue
        kind = layer % 3
        if kind == 0:
            if not DEBUG.get("skip_mix"):
                pool_layer(layer, pool_j)
            pool_j += 1
        elif kind == 2:
            conv_layer(layer)
        else:
            attn_layer(layer)
        if not DEBUG.get("skip_ffn"):
            ffn(layer)

    reg.reset()
    ystage = [reg.alloc([128, D], F32, f"ystage{i}") for i in range(2)]

    def store_block(dst_ap, nrows, col0, i):
        st = ystage[i % 2]
        for c0 in range(0, NCH, 4):
            b = next_bank()
            for c in range(c0, c0 + 4):
                sch.op("pe", call("transpose",
                    out=PS[b][0:nrows, (c - c0) * 128:(c - c0 + 1) * 128], in_=XT[:, c, col0:col0 + nrows],
                    identity=ident_f[:, :]),
                    reads=["XT", "ident_f"], writes=[f"ps{b}"], signal=(c == c0 + 3))
            if c0 == 0:
                sch.op("dve", call("tensor_copy", out=st[0:nrows, 0:512], in_=PS[b][0:nrows, :]),
                       reads=[f"ps{b}"], writes=[f"ystage{i % 2}"])
            else:
                sch.op("act", call("copy", out=st[0:nrows, 512:1024], in_=PS[b][0:nrows, :]),
                       reads=[f"ps{b}"], writes=[f"ystage{i % 2}"])
        sch.dma("sp", call("dma_start", out=dst_ap, in_=st[0:nrows, :]), reads=[f"ystage{i % 2}"], writes=["y"])

    for i in range(S // 128):
        store_block(y_p[i * 128:(i + 1) * 128, :], 128, i * 128, i)
    store_block(y_s[:, :], NS, S, 16)

    sch.finish()
    sch.emit()
    return nc, sch


def make_consts():
    ident = np.eye(128, dtype=np.float32)
    invcnt = np.zeros((128, 4, 16), np.float32)
    for g in range(4):
        w = 2 << g
        for t in range(16):
            invcnt[:, g, t] = 1.0 / min(t + 1, w)
    return ident, invcnt


def t5_bucket_np(dist):
    dist = np.asarray(dist, np.int64)
    df = np.maximum(dist, 1).astype(np.float32)
    large = 16 + (np.log(df / np.float32(16)) / np.float32(np.log(2048 / 16)) * np.float32(16)).astype(np.int32)
    large = np.minimum(large, 31)
    return np.where(dist < 16, dist, large)


def make_ohf():
    ohf = np.zeros((32, 3, 384), np.float32)
    for g, dil in enumerate((1, 4, 16)):
        steps = np.arange(129)
        bk = t5_bucket_np(dil * steps)
        for st_, b in zip(steps, bk):
            ohf[b, g, 128 + st_] = 1.0
    return ohf


def make_sample_tables():
    ohm = np.zeros((32, 3, 4, 128), np.float32)
    ohx = np.zeros((32, 3, 4, NS), np.float32)
    for t in range(4):
        for i in range(128):
            j = 124 + t - i
            if 0 <= j <= 128:
                ohm[t5_bucket_np(j), 0, t, i] = 1.0
            ohm[t5_bucket_np(4 * (127 - i)), 1, t, i] = 1.0
            ohm[t5_bucket_np(16 * (127 - i)), 2, t, i] = 1.0
    for bt in range(NS):
        t = bt % 4
        for r in range(4):
            if r >= t:
                ohx[t5_bucket_np(128 + t - r), 0, r, bt] = 1.0
            if r == t:
                ohx[t5_bucket_np(4 * 128), 1, r, bt] = 1.0
                ohx[t5_bucket_np(16 * 128), 2, r, bt] = 1.0
    selc = np.zeros((128, NS, NS), np.float32)
    for bt in range(NS):
        selc[:, bt, bt] = 1.0
    selr = np.zeros((NS, NS, 128), np.float32)
    for bt in range(NS):
        selr[bt, bt, :] = 1.0
    return ohm, ohx, selc, selr


def make_in_maps(inp, n_cores=NCORES):
    ident, invcnt = make_consts()
    ohf = make_ohf()
    ohm, ohx, selc, selr = make_sample_tables()
    vec = np.zeros((NVEC, D), np.float32)
    vec[V_GAIN:V_GAIN + 16] = inp["norm_gains"].reshape(16, D)
    vec[V_PSCALE:V_PSCALE + 2] = inp["pool_scale"]
    vec[V_B1:V_B1 + 2] = inp["conv_b_pw1"].reshape(2, D)
    vec[V_BDW] = inp["conv_b_dw"][0]
    vec[V_LNG] = inp["conv_ln_g"][0]
    vec[V_LNB] = inp["conv_ln_b"][0]
    vec[V_B2] = inp["conv_b_pw2"][0]
    vec[V_WDW:V_WDW + 31] = inp["conv_w_dw"][0]
    maps = []
    for c in range(n_cores):
        m = {
            "x_p": np.ascontiguousarray(inp["x_prompt"][c]),
            "x_s": np.ascontiguousarray(inp["x_sample"][4 * c:4 * c + 4].reshape(NS, D)),
            "st_pool": np.ascontiguousarray(inp["state_pool"][:, 4 * c:4 * c + 4]),
            "vec_rows": vec,
            "ident_f": ident,
            "invcnt": invcnt,
            "pool_w": np.ascontiguousarray(inp["pool_w"]),
            "st_conv": np.ascontiguousarray(inp["state_conv"][0, 4 * c:4 * c + 4]),
            "w_pw1": np.ascontiguousarray(inp["conv_w_pw1"][0]),
            "w_pw2": np.ascontiguousarray(inp["conv_w_pw2"][0]),
            "w_qkv": np.ascontiguousarray(inp["attn_w_qkv"][0]),
            "w_o": np.ascontiguousarray(inp["attn_w_o"][0]),
            "rel_bias": np.ascontiguousarray(inp["rel_bias"]),
            "ohf": ohf,
            "ohm": ohm,
            "ohx": ohx,
            "selc": selc,
            "selr": selr,
            "cache_kv0": np.ascontiguousarray(inp["cache_kv_g0"][0, 4 * c:4 * c + 4].reshape(4, 128, 2048)),
            "cache_kv1": np.ascontiguousarray(inp["cache_kv_g1"][0, 4 * c:4 * c + 4].reshape(4, 512, 2048)),
            "cache_kv2": np.ascontiguousarray(inp["cache_kv_g2"][0, 4 * c:4 * c + 4].reshape(4, 2048, 2048)),
            "w_up": np.ascontiguousarray(inp["ffn_w_up"]),
            "w_down": np.ascontiguousarray(inp["ffn_w_down"]),
        }
        maps.append(m)
    return maps


_PROG = {}


def run(inp, n_cores=NCORES, layers=(0, 1, 2, 3), trace=False):
    key = tuple(layers)
    if key not in _PROG:
        _PROG[key] = build_program(layers)
    nc, _ = _PROG[key]
    maps = make_in_maps(inp, n_cores)
    res = run_bass_kernel_spmd(nc, maps, core_ids=list(range(n_cores)), trace=trace)
    return res


def kernel(**inp):
    inp = {k: np.asarray(v) for k, v in inp.items()}
    res = run(inp)
    R = res.results
    n = NCORES
    y_p = np.stack([R[c]["y_p"] for c in range(n)])
    y_s = np.concatenate([R[c]["y_s"].reshape(4, 4, D) for c in range(n)])
    pool_p = np.stack([R[c]["o_pool_p"] for c in range(n)], axis=1)
    pool_s = np.concatenate([R[c]["o_pool_s"] for c in range(n)], axis=1)
    outs = [y_p, y_s, pool_p, pool_s]
    for g, win in enumerate((128, 512, 2048)):
        kp = np.stack([R[c][f"o_kv_p{g}"].reshape(win, 2, 16, 64) for c in range(n)])[None]
        ks = np.concatenate([R[c][f"o_kv_s{g}"].reshape(4, win, 2, 16, 64) for c in range(n)])[None]
        outs += [kp, ks]
    conv_p = np.stack([R[c]["o_conv_p"] for c in range(n)])[None]
    conv_s = np.concatenate([R[c]["o_conv_s"] for c in range(n)])[None]
    outs += [conv_p, conv_s]
    return tuple(np.ascontiguousarray(o, dtype=np.float32) for o in outs)
```
